# Optimizing a Trainium2 kernel written in Bass

```python
import jax
import jax.numpy as jnp
from jax import lax
import numpy as np


D_MODEL = 1024
BATCH = 16
SEQ = 2048
DEPTH = 2

N_MIXERS = 2
N_ATTN_LAYERS = (DEPTH + 1) // 2
N_MLSTM_LAYERS = DEPTH // 2

DIL_CONFIGS = ((128, 1), (512, 4), (2048, 16))
N_GROUPS = len(DIL_CONFIGS)
ATTN_HEADS = 8
ATTN_HEAD_DIM = 128
ATTN_WIDTH = ATTN_HEADS * ATTN_HEAD_DIM
ATTN_IN_WIDTH = N_GROUPS * 3 * ATTN_WIDTH
ROT_DIM = ATTN_HEAD_DIM // 4
ROPE_THETA = 500000.0
BLOCK = 128

MLSTM_HEADS = 8
MLSTM_QK_DIM = D_MODEL // 2 // MLSTM_HEADS
MLSTM_V_DIM = D_MODEL // MLSTM_HEADS
QK_WIDTH = MLSTM_HEADS * MLSTM_QK_DIM
MLSTM_IN_WIDTH = 2 * QK_WIDTH + 2 * D_MODEL + 2 * MLSTM_HEADS
MLSTM_CHUNK = 64
CONV_WIDTH = 4

D_FF = -(-8 * D_MODEL // (3 * 256)) * 256

RMS_EPS = 1e-6

kernel_name = 'hybrid_dilated_attn_mlstm'


def rms_norm(x, g):
    xf = x.astype(jnp.float32)
    y = xf * lax.rsqrt(jnp.mean(xf * xf, axis=-1, keepdims=True) + RMS_EPS)
    return y.astype(x.dtype) * g.astype(x.dtype)


def apply_partial_rope(t, positions):
    half = ROT_DIM // 2
    inv_freq = ROPE_THETA ** (-jnp.arange(half, dtype=jnp.float32) * 2.0 / ROT_DIM)
    ang = positions.astype(jnp.float32)[..., None] * inv_freq
    cos = jnp.cos(ang)[:, :, None, :]
    sin = jnp.sin(ang)[:, :, None, :]
    tf = t.astype(jnp.float32)
    x1 = tf[..., :half]
    x2 = tf[..., half:ROT_DIM]
    out = jnp.concatenate([x1 * cos - x2 * sin, x2 * cos + x1 * sin, tf[..., ROT_DIM:]], axis=-1)
    return out.astype(t.dtype)


def banded_window_attention(q, k, v, window):
    *lead, L, hd = q.shape
    nb = -(-L // BLOCK)
    Lp = nb * BLOCK
    pad = [(0, 0)] * len(lead) + [(0, Lp - L), (0, 0)]

    def blocks(t):
        return jnp.pad(t, pad).reshape(*lead, nb, BLOCK, hd)

    def with_prev(t):
        prev = jnp.concatenate([jnp.zeros_like(t[..., :1, :, :]), t[..., :-1, :, :]], axis=-3)
        return jnp.concatenate([prev, t], axis=-2)

    qb = blocks(q)
    kk = with_prev(blocks(k))
    vv = with_prev(blocks(v))
    s = jnp.einsum('...iqd,...ikd->...iqk', qb, kk).astype(jnp.float32)
    blk = jnp.arange(nb)[:, None, None]
    a = jnp.arange(BLOCK)[None, :, None]
    c = jnp.arange(2 * BLOCK)[None, None, :]
    dist = a + BLOCK - c
    valid = (dist >= 0) & (dist <= window) & ((blk - 1) * BLOCK + c >= 0)
    s = jnp.where(valid, s, -jnp.inf)
    m = jnp.max(s, axis=-1, keepdims=True)
    p = jnp.exp(s - m)
    den = jnp.sum(p, axis=-1, keepdims=True)
    o = jnp.einsum('...iqk,...ikd->...iqd', (p / den).astype(v.dtype), vv)
    lse = (m + jnp.log(den))[..., 0]
    o = o.reshape(*lead, Lp, hd)[..., :L, :]
    lse = lse.reshape(*lead, Lp)[..., :L]
    return o, lse


def dilated_window_attention(q, k, v, window, dilation):
    B, S, H, hd = q.shape
    L = S // dilation

    def to_sub(t):
        return t.reshape(B, L, dilation, H, hd).transpose(0, 2, 3, 1, 4)

    o, lse = banded_window_attention(to_sub(q), to_sub(k), to_sub(v), window // dilation)
    o = o.transpose(0, 3, 1, 2, 4).reshape(B, S, H, hd)
    lse = lse.transpose(0, 3, 1, 2).reshape(B, S, H)
    return o, lse


def dilated_attention_mixer(xn, positions, w_in, w_out):
    B, S, _ = xn.shape
    proj = (xn @ w_in).reshape(B, S, N_GROUPS, 3, ATTN_HEADS, ATTN_HEAD_DIM)
    scale = ATTN_HEAD_DIM ** -0.5
    outs = []
    lses = []
    for g, (window, dilation) in enumerate(DIL_CONFIGS):
        q = apply_partial_rope(proj[:, :, g, 0], positions) * scale
        k = apply_partial_rope(proj[:, :, g, 1], positions)
        v = proj[:, :, g, 2]
        o, lse = dilated_window_attention(q, k, v, window, dilation)
        outs.append(o)
        lses.append(lse)
    wts = jax.nn.softmax(jnp.stack(lses), axis=0)
    o = jnp.einsum('gbsh,gbshd->bshd', wts.astype(xn.dtype), jnp.stack(outs))
    return o.reshape(B, S, ATTN_WIDTH) @ w_out


def causal_depthwise_conv(t, w, b):
    out = lax.conv_general_dilated(
        t, w[:, None, :].astype(t.dtype), window_strides=(1,),
        padding=[(CONV_WIDTH - 1, 0)], dimension_numbers=('NWC', 'WIO', 'NWC'),
        feature_group_count=t.shape[-1])
    return out + b.astype(t.dtype)


def mlstm_chunk_step(carry, xs):
    C, n, m = carry
    q, k, v, ig, lf = xs
    L = q.shape[2]
    b = jnp.cumsum(lf, axis=-1)
    causal = jnp.tril(jnp.ones((L, L), dtype=bool))
    dmat = jnp.where(causal, b[..., :, None] - b[..., None, :] + ig[..., None, :], -jnp.inf)
    inter = b + m[..., None]
    m_t = jnp.maximum(inter, jnp.max(dmat, axis=-1))
    w_intra = jnp.exp(dmat - m_t[..., None])
    w_inter = jnp.exp(inter - m_t)
    sm = w_intra * jnp.einsum('bhtd,bhsd->bhts', q, k)
    num = jnp.einsum('bhts,bhsv->bhtv', sm, v) + w_inter[..., None] * jnp.einsum('bhtd,bhdv->bhtv', q, C)
    den = jnp.sum(sm, axis=-1) + w_inter * jnp.einsum('bhtd,bhd->bht', q, n)
    h = num / jnp.maximum(jnp.abs(den), jnp.exp(-m_t))[..., None]
    b_last = b[..., -1]
    decay = b_last[..., None] - b + ig
    m_new = jnp.maximum(b_last + m, jnp.max(decay, axis=-1))
    ws = jnp.exp(decay - m_new[..., None])
    carry_scale = jnp.exp(b_last + m - m_new)
    C_new = carry_scale[..., None, None] * C + jnp.einsum('bhs,bhsd,bhsv->bhdv', ws, k, v)
    n_new = carry_scale[..., None] * n + jnp.einsum('bhs,bhsd->bhd', ws, k)
    return (C_new, n_new, m_new), h


def mlstm_chunkwise(q, k, v, ig, lf):
    B, H, S, dk = q.shape
    dv = v.shape[-1]
    nc = S // MLSTM_CHUNK

    def to_chunks(t):
        return jnp.moveaxis(t.reshape(B, H, nc, MLSTM_CHUNK, *t.shape[3:]), 2, 0)

    init = (jnp.zeros((B, H, dk, dv), jnp.float32), jnp.zeros((B, H, dk), jnp.float32),
            jnp.zeros((B, H), jnp.float32))
    _, h = lax.scan(mlstm_chunk_step, init,
                    (to_chunks(q), to_chunks(k), to_chunks(v), to_chunks(ig), to_chunks(lf)))
    return jnp.moveaxis(h, 0, 2).reshape(B, H, S, dv)


def mlstm_mixer(xn, w_in, conv_w, conv_b, ig_bias, fg_bias, head_gain, w_out):
    B, S, _ = xn.shape
    proj = xn @ w_in
    qk_pre = proj[..., :2 * QK_WIDTH]
    v = proj[..., 2 * QK_WIDTH:2 * QK_WIDTH + D_MODEL]
    o_pre = proj[..., 2 * QK_WIDTH + D_MODEL:2 * QK_WIDTH + 2 * D_MODEL]
    gates = proj[..., 2 * QK_WIDTH + 2 * D_MODEL:].astype(jnp.float32)
    qk = jax.nn.silu(causal_depthwise_conv(qk_pre, conv_w, conv_b))
    ig = gates[..., :MLSTM_HEADS] + ig_bias.astype(jnp.float32)
    lf = jax.nn.log_sigmoid(gates[..., MLSTM_HEADS:] + fg_bias.astype(jnp.float32))

    def heads(t, d):
        return t.reshape(B, S, MLSTM_HEADS, d).transpose(0, 2, 1, 3).astype(jnp.float32)

    q = heads(qk[..., :QK_WIDTH], MLSTM_QK_DIM)
    k = heads(qk[..., QK_WIDTH:], MLSTM_QK_DIM) * (MLSTM_QK_DIM ** -0.5)
    vh = heads(v, MLSTM_V_DIM)
    h = mlstm_chunkwise(q, k, vh, ig.transpose(0, 2, 1), lf.transpose(0, 2, 1))
    h = h.transpose(0, 2, 1, 3)
    h = h * lax.rsqrt(jnp.mean(h * h, axis=-1, keepdims=True) + RMS_EPS)
    h = h.reshape(B, S, D_MODEL).astype(xn.dtype) * head_gain.astype(xn.dtype)
    return (h * jax.nn.sigmoid(o_pre)) @ w_out


def swiglu(xn, w_in, w_out):
    gu = xn @ w_in
    return (jax.nn.silu(gu[..., :D_FF]) * gu[..., D_FF:]) @ w_out


def setup_inputs(seed: int = 0) -> dict:
    key = jax.random.key(seed)
    ks = jax.random.split(key, 20)

    def dense(k, shape, fan_in):
        return jax.random.normal(k, shape, jnp.float32) * (fan_in ** -0.5)

    def gain(k, shape):
        return 1.0 + 0.05 * jax.random.normal(k, shape, jnp.float32)

    x = jax.random.normal(ks[0], (BATCH, SEQ, D_MODEL), jnp.float32)
    offset = jax.random.randint(ks[1], (BATCH, 1), 0, 4096, dtype=jnp.int32)
    positions = offset + jnp.arange(SEQ, dtype=jnp.int32)[None, :]
    return {
        'x': x,
        'positions': positions,
        'attn_norm': gain(ks[2], (N_ATTN_LAYERS, D_MODEL)),
        'attn_w_in': dense(ks[3], (N_ATTN_LAYERS, D_MODEL, ATTN_IN_WIDTH), D_MODEL),
        'attn_w_out': dense(ks[4], (N_ATTN_LAYERS, ATTN_WIDTH, D_MODEL), ATTN_WIDTH),
        'mlstm_norm': gain(ks[5], (N_MLSTM_LAYERS, D_MODEL)),
        'mlstm_w_in': dense(ks[6], (N_MLSTM_LAYERS, D_MODEL, MLSTM_IN_WIDTH), D_MODEL),
        'mlstm_conv_w': dense(ks[7], (N_MLSTM_LAYERS, CONV_WIDTH, 2 * QK_WIDTH), CONV_WIDTH),
        'mlstm_conv_b': 0.02 * jax.random.normal(ks[8], (N_MLSTM_LAYERS, 2 * QK_WIDTH), jnp.float32),
        'mlstm_ig_bias': 0.1 * jax.random.normal(ks[9], (N_MLSTM_LAYERS, MLSTM_HEADS), jnp.float32),
        'mlstm_fg_bias': 3.0 + 0.1 * jax.random.normal(ks[10], (N_MLSTM_LAYERS, MLSTM_HEADS), jnp.float32),
        'mlstm_head_gain': gain(ks[11], (N_MLSTM_LAYERS, D_MODEL)),
        'mlstm_w_out': dense(ks[12], (N_MLSTM_LAYERS, D_MODEL, D_MODEL), D_MODEL),
        'ffn_norm': gain(ks[13], (DEPTH, D_MODEL)),
        'ffn_w_in': dense(ks[14], (DEPTH, D_MODEL, 2 * D_FF), D_MODEL),
        'ffn_w_out': dense(ks[15], (DEPTH, D_FF, D_MODEL), D_FF),
        'final_norm': gain(ks[16], (D_MODEL,)),
    }


def reference(x, positions, attn_norm, attn_w_in, attn_w_out, mlstm_norm, mlstm_w_in,
              mlstm_conv_w, mlstm_conv_b, mlstm_ig_bias, mlstm_fg_bias, mlstm_head_gain,
              mlstm_w_out, ffn_norm, ffn_w_in, ffn_w_out, final_norm):
    h = x
    for i in range(DEPTH):
        j = i // N_MIXERS
        if i % N_MIXERS == 0:
            h = h + dilated_attention_mixer(rms_norm(h, attn_norm[j]), positions,
                                            attn_w_in[j], attn_w_out[j])
        else:
            h = h + mlstm_mixer(rms_norm(h, mlstm_norm[j]), mlstm_w_in[j], mlstm_conv_w[j],
                                mlstm_conv_b[j], mlstm_ig_bias[j], mlstm_fg_bias[j],
                                mlstm_head_gain[j], mlstm_w_out[j])
        h = h + swiglu(rms_norm(h, ffn_norm[i]), ffn_w_in[i], ffn_w_out[i])
    return rms_norm(h, final_norm)
```

```python
import contextlib
import os
import numpy as np
import concourse.bass as bass
import concourse.mybir as mybir
from concourse.bass_utils import run_bass_kernel_spmd

F32 = mybir.dt.float32
BF16 = mybir.dt.bfloat16
I32 = mybir.dt.int32
AF = mybir.ActivationFunctionType
ALU = mybir.AluOpType
AX = mybir.AxisListType

COMPUTE = ("pe", "act", "dve", "pool")
DMAQ = ("sp", "actq", "poolq")
STREAM_OF = {"pe": "pe", "act": "act", "dve": "dve", "pool": "pool", "sp": "sp", "actq": "act", "poolq": "pool"}


class _Op:
    __slots__ = ("eng", "fn", "deps", "dma", "semkey", "signal", "count", "idx")


class Sched:
    def __init__(self, nc):
        self.nc = nc
        self.ops = []
        self.res_w = {}
        self.res_r = {}

    def add(self, eng, fn, reads=(), writes=(), semkey=None):
        idx = len(self.ops)
        deps = set()
        bl = self.__dict__.setdefault("bank_last", {})
        for r in list(reads) + list(writes):
            if isinstance(r, tuple) and r[0] == "bank":
                last = bl.setdefault(r, {})
                for e2, i2 in last.items():
                    if STREAM_OF[e2] != STREAM_OF[eng]:
                        deps.add(i2)
                last[eng] = idx
        reads = [r for r in reads if not (isinstance(r, tuple) and r[0] == "bank")]
        writes = [r for r in writes if not (isinstance(r, tuple) and r[0] == "bank")]
        for r in reads:
            w = self.res_w.get(r)
            if w is not None:
                deps.add(w)
        for wr in writes:
            w = self.res_w.get(wr)
            if w is not None:
                deps.add(w)
            lastrd = {}
            for rd in self.res_r.get(wr, ()):
                rop = self.ops[rd]
                if rop.dma:
                    deps.add(rd)
                else:
                    lastrd[rop.eng] = rd
            deps.update(lastrd.values())
        for r in reads:
            self.res_r.setdefault(r, []).append(idx)
        for wr in writes:
            self.res_w[wr] = idx
            self.res_r[wr] = []
        op = _Op()
        op.eng = eng
        op.fn = fn
        op.dma = eng in DMAQ
        op.semkey = semkey if op.dma else None
        if op.dma:
            assert semkey is not None
        op.deps = deps
        op.signal = op.dma
        op.count = 0
        op.idx = idx
        self.ops.append(op)
        return idx

    def pe(self, fn, reads=(), writes=()):
        return self.add("pe", fn, reads, writes)

    def act(self, fn, reads=(), writes=()):
        return self.add("act", fn, reads, writes)

    def dve(self, fn, reads=(), writes=()):
        return self.add("dve", fn, reads, writes)

    def pool(self, fn, reads=(), writes=()):
        return self.add("pool", fn, reads, writes)

    def dma(self, q, out, in_, reads=(), writes=(), semkey=None, **kw):
        def fn(e, out=out, in_=in_, kw=kw):
            return e.dma_start(out=out, in_=in_, **kw)
        return self.add(q, fn, reads, writes, semkey=semkey)

    def emit(self, final_wait_stream="sp"):
        nc = self.nc
        ops = self.ops
        needed = []
        for op in ops:
            nd = []
            for d in op.deps:
                dop = ops[d]
                same_stream = STREAM_OF[dop.eng] == STREAM_OF[op.eng]
                if dop.dma:
                    nd.append(d)
                elif same_stream:
                    if dop.eng != "pe":
                        nd.append(d)
                else:
                    nd.append(d)
            needed.append(nd)
            for d in nd:
                ops[d].signal = True
        cnt = {}
        for op in ops:
            if op.signal:
                key = ("dma", op.semkey) if op.dma else ("eng", op.eng)
                inc = 16 if op.dma else 1
                cnt[key] = cnt.get(key, 0) + inc
                op.count = cnt[key]
        keys = list(cnt.keys())
        stack = contextlib.ExitStack()
        sems = {}
        with stack:
            for k in keys:
                sems[k] = stack.enter_context(nc.semaphore("s_%s_%s" % (k[0], str(k[1]).replace(" ", ""))))
            block = stack.enter_context(nc.Block())
            streams = {"pe": [], "act": [], "dve": [], "pool": [], "sp": []}
            for op in ops:
                streams[STREAM_OF[op.eng]].append(op)

            def build(stream_name, eng):
                waited = {}
                for op in streams[stream_name]:
                    for d in needed[op.idx]:
                        dop = ops[d]
                        key = ("dma", dop.semkey) if dop.dma else ("eng", dop.eng)
                        if waited.get(key, 0) >= dop.count:
                            continue
                        eng.wait_ge(sems[key], dop.count)
                        waited[key] = dop.count
                    ins = op.fn(eng)
                    if op.signal:
                        key = ("dma", op.semkey) if op.dma else ("eng", op.eng)
                        ins.then_inc(sems[key], 16 if op.dma else 1)
                if stream_name == final_wait_stream:
                    for k in keys:
                        if k[0] == "dma":
                            eng.wait_ge(sems[k], cnt[k])

            @block.tensor
            def _(e):
                build("pe", e)

            @block.scalar
            def _(e):
                build("act", e)

            @block.vector
            def _(e):
                build("dve", e)

            @block.gpsimd
            def _(e):
                build("pool", e)

            @block.sync
            def _(e):
                build("sp", e)


def _sched_barrier(self):
    last = getattr(self, "_bar_last", {})
    dmas = []
    for op in self.ops[getattr(self, "_bar_start", 0):]:
        if op.dma:
            dmas.append(op.idx)
        else:
            last[op.eng] = op.idx
    self._bar_last = last
    self._bar_start = len(self.ops)
    self._bar_deps = set(last.values()) | set(dmas)
    self._bar_seen = set()


_orig_add = Sched.add


def _add_with_barrier(self, eng, fn, reads=(), writes=(), semkey=None):
    idx = _orig_add(self, eng, fn, reads, writes, semkey)
    bd = getattr(self, "_bar_deps", None)
    if bd:
        st = STREAM_OF[eng]
        if st not in self._bar_seen:
            self._bar_seen.add(st)
            self.ops[idx].deps |= bd
    return idx


Sched.add = _add_with_barrier
Sched.barrier = _sched_barrier


class Arena:
    def __init__(self, ap_f32, nwords):
        self.base = ap_f32
        self.n = nwords
        self.off = 0

    def mark(self):
        return self.off

    def reset(self, m):
        self.off = m

    def alloc(self, free_shape, dt):
        n = 1
        for v in free_shape:
            n *= v
        words = (n * (2 if dt == BF16 else 4) + 3) // 4
        words = (words + 7) // 8 * 8
        assert self.off + words <= self.n, ("SBUF arena overflow", self.off, words, self.n)
        v = self.base[:, self.off:self.off + words]
        self.off += words
        if dt == BF16:
            v = v.bitcast(BF16)[:, 0:n]
        elif dt == I32:
            v = v.bitcast(I32)[:, 0:n]
        else:
            v = v[:, 0:n]
        if len(free_shape) == 2:
            v = v.rearrange("p (a b) -> p a b", a=free_shape[0])
        elif len(free_shape) == 3:
            v = v.rearrange("p (a b c) -> p a b c", a=free_shape[0], b=free_shape[1])
        return v


DIL = ((1, 16), (4, 4), (16, 1))
TWO_PI = 6.283185307179586
C1 = 6.28125
C2 = TWO_PI - C1
PI = 3.141592653589793
DFF = 2816
NFC = 22


def build_program(n_stage=99, dbg=None):
    nc = bass.Bass("TRN2", target_bir_lowering=False)

    def din(name, shape, dt=F32):
        return nc.dram_tensor(name, shape, dt, kind="ExternalInput").ap()

    x_d = din("x", [4096, 1024])
    pos_d = din("pos", [2, 2048], I32)
    attn_norm = din("attn_norm", [1, 1024])
    attn_w_in = din("attn_w_in", [1024, 9216])
    attn_w_out = din("attn_w_out", [1024, 1024])
    mlstm_norm = din("mlstm_norm", [1, 1024])
    mlstm_w_in = din("mlstm_w_in", [1024, 3088])
    conv_w = din("conv_w", [4, 1024])
    conv_b = din("conv_b", [1, 1024])
    ig_bias = din("ig_bias", [1, 8])
    fg_bias = din("fg_bias", [1, 8])
    head_gain = din("head_gain", [1, 1024])
    mlstm_w_out = din("mlstm_w_out", [1024, 1024])
    ffn_norm = din("ffn_norm", [2, 1024])
    ffn_w_in = din("ffn_w_in", [2, 1024, 2 * DFF])
    ffn_w_out = din("ffn_w_out", [2, DFF, 1024])
    final_norm = din("final_norm", [1, 1024])
    ident_d = din("ident", [128, 128])
    mask_d = din("mask2", [128, 256])
    invf_d = din("invf", [128, 16])

    out_d = nc.dram_tensor("out", [4096, 1024], F32, kind="ExternalOutput").ap()
    dkind = "ExternalOutput" if dbg else "Internal"
    oT_d = nc.dram_tensor("oT_d", [2, 8, 128, 2048], BF16, kind=dkind).ap()
    hA_d = nc.dram_tensor("hA_d", [4096, 1024], F32, kind=dkind).ap()
    hB_d = nc.dram_tensor("hB_d", [4096, 1024], F32, kind=dkind).ap()

    S = Sched(nc)
    NW = 52224
    stack = contextlib.ExitStack()
    with stack:
        arena_t = stack.enter_context(nc.sbuf_tensor("arena", [128, NW], F32))
        banks = [stack.enter_context(nc.psum_tensor("bank%d" % i, [128, 512], F32)) for i in range(8)]
        A = Arena(arena_t[:], NW)

        def bank_bf(i):
            return banks[i][:].bitcast(BF16)

        identf = A.alloc([128], F32)
        identb = A.alloc([128], BF16)
        maskf = A.alloc([256], F32)
        maskb = A.alloc([256], BF16)
        onesb = A.alloc([128], BF16)
        gbc = A.alloc([1024], F32)
        xts = [A.alloc([1024], F32) for _ in range(2)]
        xnb = [A.alloc([1024], BF16) for _ in range(2)]
        stat = A.alloc([2, 8], F32)
        S.dma("sp", identf, ident_d, writes=["identf"], semkey="c_ident")
        S.dma("sp", maskf, mask_d, writes=["maskf"], semkey="c_mask")
        S.dve(lambda e: e.tensor_copy(out=identb, in_=identf), reads=["identf"], writes=["identb"])
        S.dve(lambda e: e.tensor_copy(out=maskb, in_=maskf), reads=["maskf"], writes=["maskb"])
        S.dve(lambda e: e.memset(onesb, 1.0), writes=["onesb"])
        mark0 = A.mark()


        def load_cast(dst, src, name, nsplit=4):
            n = dst.shape[1]
            step = (n + nsplit - 1) // nsplit
            res = []
            for i, lo in enumerate(range(0, n, step)):
                hi = min(n, lo + step)
                S.dma("poolq", dst[:, lo:hi], src[:, lo:hi], writes=[(name, i)], semkey=(name, i))
                res.append((name, i))
            return res

        def load_gamma(g_ap):
            S.dma("sp", gbc, g_ap.partition_broadcast(128), writes=["gbc"], semkey="gbc")

        def norm_core(src, src_res, slot, out_ap, out_res):
            ss = stat[:, slot, 0:1]
            vv = stat[:, slot, 1:2]
            rs = stat[:, slot, 2:3]
            sres = ("stat", slot)
            S.act(lambda e: e.activation(out=xnb[slot], in_=src, func=AF.Square, accum_out=ss),
                  reads=[src_res], writes=[sres, ("xnb", slot)])
            S.dve(lambda e: e.tensor_scalar(out=vv, in0=ss, scalar1=1.0 / 1024, scalar2=1e-6, op0=ALU.mult, op1=ALU.add),
                  reads=[sres], writes=[(sres, "v")])
            S.pool(lambda e: e.tensor_tensor(out=rs, in0=vv, in1=mhalf[:, 0:1], op=ALU.pow),
                   reads=[(sres, "v"), "mhalf"], writes=[(sres, "r")])
            S.dve(lambda e: e.scalar_tensor_tensor(out=out_ap, in0=src, scalar=rs, in1=gbc, op0=ALU.mult, op1=ALU.mult),
                  reads=[src_res, (sres, "r"), "gbc"], writes=[out_res])

        mhalf = A.alloc([8], F32)
        S.dve(lambda e: e.memset(mhalf, -0.5), writes=["mhalf"])
        mark0 = A.mark()
        TB = 7

        def norm_transpose(src, src_res, slot, dstT, dst_res, defer=False):
            xb = xnb[slot]
            norm_core(src, src_res, slot, xb, ("xnb", slot))

            def part2():
                pt = bank_bf(TB)
                for k in range(8):
                    S.pe(lambda e, k=k: e.transpose(out=pt[:, k * 128:(k + 1) * 128], in_=xb[:, k * 128:(k + 1) * 128], identity=identb),
                         reads=[("xnb", slot), "identb"], writes=[("bank", TB)])
                S.act(lambda e: e.copy(out=dstT, in_=pt.rearrange("p (k t) -> p k t", k=8)),
                      reads=[("bank", TB)], writes=[dst_res])
            if defer:
                return part2
            part2()
            return None

        def stage_A():
            A.reset(mark0)
            xnTs = [A.alloc([8, 2048], BF16) for _ in range(2)]
            xa = [A.alloc([1024], F32) for _ in range(4)]
            numT = A.alloc([2048], F32)
            denT = A.alloc([2048], F32)
            oTb = A.alloc([2048], BF16)
            wt = [A.alloc([8, 384], BF16) for _ in range(2)]
            cs = [A.alloc([2, 32, 16], F32) for _ in range(3)]
            vtok = [A.alloc([16, 128], BF16) for _ in range(2)]
            qkb = [A.alloc([16, 256], BF16) for _ in range(2)]
            qkf = [A.alloc([16, 2, 32], F32) for _ in range(2)]
            qk_region = A.alloc([8192], BF16)
            QT = [qk_region[:, 0:2048], qk_region[:, 2048:4096]]
            KT = [qk_region[:, 4096:6144], qk_region[:, 6144:8192]]
            tmp = [A.alloc([16, 2, 16], F32) for _ in range(4)]
            pT = [A.alloc([256], BF16) for _ in range(6)]
            posi = A.alloc([32], I32)
            posf = A.alloc([32], F32)
            invf = A.alloc([16], F32)
            def _v(ap, lo, dt):
                v = ap[:, lo:lo + 512]
                if dt == I32:
                    v = v.bitcast(I32)
                return v.rearrange("p (a b) -> p a b", a=32)
            ang = _v(xa[0], 0, F32)
            rr = _v(xa[0], 512, F32)
            r2 = _v(xa[1], 0, F32)
            nf = _v(xa[1], 512, F32)
            ni = _v(xa[2], 0, I32)
            mk = _v(xa[2], 512, F32)

            print('arena A', A.off * 4 / 1024, flush=True)
            load_gamma(attn_norm)
            S.dma("sp", invf, invf_d, writes=["invf"], semkey="c_invf")
            posrow = qk_region.bitcast(F32)
            posrow_i = qk_region.bitcast(I32)
            onef1 = A.alloc([8], F32)
            S.dma("sp", posrow_i[0:1, :], pos_d.rearrange("(o s) t -> o (s t)", o=1), writes=["posrow_i"], semkey="c_pos")
            S.dve(lambda e: e.tensor_copy(out=posrow[0:1, :], in_=posrow_i[0:1, :]), reads=["posrow_i"], writes=["posrow"])
            S.dve(lambda e: e.memset(onef1, 1.0), writes=["onef1"])
            for g, (d, nb) in enumerate(DIL[:int(os.environ.get('KA_ROPE', '3'))]):
                for s in range(2):
                    for tau in range(16):
                        r_, b_ = divmod(tau, nb)
                        base = s * 2048 + 128 * b_ * d + r_
                        S.pe(lambda e, s=s, tau=tau, base=base, d=d: e.matmul(
                            banks[0][:, s * 16 + tau: s * 16 + tau + 1], lhsT=posrow[0:1, base: base + 127 * d + 1: d], rhs=onef1[0:1, 0:1],
                            start=True, stop=True), reads=["posrow", "onef1"], writes=[("bank", 0)])
                S.dve(lambda e: e.tensor_copy(out=posf, in_=banks[0][:, 0:32]), reads=[("bank", 0)], writes=["posf"])
                R = ["rope_tmp", ("xa", 0), ("xa", 1), ("xa", 2)]
                S.dve(lambda e: e.tensor_tensor(out=ang, in0=posf.unsqueeze(2).to_broadcast([128, 32, 16]),
                                                in1=invf.unsqueeze(1).to_broadcast([128, 32, 16]), op=ALU.mult),
                      reads=R + ["invf", "posf"], writes=R)
                S.dve(lambda e: e.tensor_scalar(out=nf, in0=ang, scalar1=1.0 / TWO_PI, scalar2=None, op0=ALU.mult), reads=R, writes=R)
                S.dve(lambda e: e.tensor_copy(out=ni, in_=nf), reads=R, writes=R)
                S.dve(lambda e: e.tensor_copy(out=nf, in_=ni), reads=R, writes=R)
                S.dve(lambda e: e.scalar_tensor_tensor(out=rr, in0=nf, scalar=-C1, in1=ang, op0=ALU.mult, op1=ALU.add), reads=R, writes=R)
                S.dve(lambda e: e.scalar_tensor_tensor(out=rr, in0=nf, scalar=-C2, in1=rr, op0=ALU.mult, op1=ALU.add), reads=R, writes=R)

                def wrap(t):
                    S.dve(lambda e: e.tensor_single_scalar(out=mk, in_=t, scalar=PI, op=ALU.is_gt), reads=R, writes=R)
                    S.dve(lambda e: e.scalar_tensor_tensor(out=t, in0=mk, scalar=-TWO_PI, in1=t, op0=ALU.mult, op1=ALU.add), reads=R, writes=R)
                    S.dve(lambda e: e.tensor_single_scalar(out=mk, in_=t, scalar=-PI, op=ALU.is_lt), reads=R, writes=R)
                    S.dve(lambda e: e.scalar_tensor_tensor(out=t, in0=mk, scalar=TWO_PI, in1=t, op0=ALU.mult, op1=ALU.add), reads=R, writes=R)
                    S.dve(lambda e: e.tensor_scalar(out=t, in0=t, scalar1=3.1415925, scalar2=-3.1415925, op0=ALU.min, op1=ALU.max), reads=R, writes=R)
                wrap(rr)
                S.dve(lambda e: e.tensor_scalar(out=r2, in0=rr, scalar1=PI / 2, scalar2=None, op0=ALU.add), reads=R, writes=R)
                wrap(r2)
                S.act(lambda e, g=g: e.activation(out=cs[g][:, 1], in_=rr, func=AF.Sin), reads=R, writes=[("cs", g)])
                S.act(lambda e, g=g: e.activation(out=cs[g][:, 0], in_=r2, func=AF.Sin), reads=R, writes=[("cs", g)])

            w_in_v = attn_w_in.rearrange("(k p) (g t h d) -> p k g t h d", p=128, g=3, t=3, h=8)
            SCALE = 128.0 ** -0.5
            iters = [(h, g) for h in range(8) for g in range(3)][:int(os.environ.get('KA_ITERS', '24'))]
            KA_LEVEL = int(os.environ.get('KA_LEVEL', '9'))

            def load_w(it):
                h, g = iters[it]
                sl = it % 2
                for t3 in range(3):
                    S.dma("poolq", wt[sl][:, :, t3 * 128:(t3 + 1) * 128], w_in_v[:, :, g, t3, h, :],
                          writes=[("wt", sl, t3)], semkey=("wt", sl, t3))

            def tok_slice(g, tau, n_tiles=1):
                d, nb = DIL[g]
                r, b = divmod(tau, nb)
                base = 128 * b * d + r
                return slice(base, base + (128 * n_tiles - 1) * d + 1, d)

            def a0_load(sq, t):
                gt = sq * 16 + t
                S.dma("sp", xa[gt % 4], x_d[gt * 128:(gt + 1) * 128, :], writes=[("xa", gt % 4)], semkey=("xa", gt % 4))

            def a0_norm(sq, t, defer):
                gt = sq * 16 + t
                return norm_transpose(xa[gt % 4], ("xa", gt % 4), gt % 2, xnTs[sq][:, :, t * 128:(t + 1) * 128], ("xnT", sq), defer=defer)

            for t in range(3):
                a0_load(0, t)
            for t in range(16):
                if t + 3 < 16:
                    a0_load(0, t + 3)
                a0_norm(0, t, False)

            for s in range(2):
                xnT = xnTs[s]
                XN = ("xnT", s)

                def P(it):
                    h, g = iters[it]
                    sl = it % 2
                    if it + 1 < len(iters):
                        load_w(it + 1)
                    for tau in range(16):
                        pb = tau % 2
                        pp = banks[pb]
                        ts_ = tok_slice(g, tau)
                        for k in range(8):
                            S.pe(lambda e, k=k, pp=pp, ts_=ts_, sl=sl, xnT=xnT: e.matmul(pp[:, 0:384], lhsT=xnT[:, k, ts_], rhs=wt[sl][:, k, :],
                                                                              start=(k == 0), stop=(k == 7)),
                                 reads=[XN, ("wt", sl, 0), ("wt", sl, 1), ("wt", sl, 2)], writes=[("bank", pb)])
                        ppv = pp[:, 0:256].rearrange("p (c e) -> p c e", c=2)
                        o_rest = qkb[sl][:, tau, :].rearrange("p (c e) -> p c e", c=2)[:, :, 32:128]
                        if tau < 6 or tau % 2 == 0:
                            S.act(lambda e, pp=pp, tau=tau: e.copy(out=vtok[sl][:, tau, :], in_=pp[:, 256:384]),
                                  reads=[("bank", pb)], writes=[("vtok", sl)])
                            S.act(lambda e, ppv=ppv, o_rest=o_rest: e.copy(out=o_rest, in_=ppv[:, :, 32:128]),
                                  reads=[("bank", pb)], writes=[("qkb", sl, "rest")])
                            S.act(lambda e, ppv=ppv, tau=tau: e.copy(out=qkf[sl][:, tau, :, :], in_=ppv[:, :, 0:32]),
                                  reads=[("bank", pb)], writes=[("qkf", sl)])
                        else:
                            S.dve(lambda e, pp=pp, tau=tau: e.tensor_copy(out=vtok[sl][:, tau, :], in_=pp[:, 256:384]),
                                  reads=[("bank", pb)], writes=[("vtok", sl)])
                            S.dve(lambda e, ppv=ppv, o_rest=o_rest: e.tensor_copy(out=o_rest, in_=ppv[:, :, 32:128]),
                                  reads=[("bank", pb)], writes=[("qkb", sl, "rest")])
                            S.dve(lambda e, ppv=ppv, tau=tau: e.tensor_copy(out=qkf[sl][:, tau, :, :], in_=ppv[:, :, 0:32]),
                                  reads=[("bank", pb)], writes=[("qkf", sl)])
                        yield
                    cosv = cs[g][:, 0, s * 16:(s + 1) * 16, :].unsqueeze(2).to_broadcast([128, 16, 2, 16])
                    sinv = cs[g][:, 1, s * 16:(s + 1) * 16, :].unsqueeze(2).to_broadcast([128, 16, 2, 16])
                    x1 = qkf[sl][:, :, :, 0:16]
                    x2 = qkf[sl][:, :, :, 16:32]
                    ob = qkb[sl].rearrange("p t (c e) -> p t c e", c=2)
                    TR = [("ropet", i) for i in range(4)]
                    S.pool(lambda e: e.tensor_tensor(out=tmp[0], in0=x1, in1=cosv, op=ALU.mult), reads=[("qkf", sl), ("cs", g)], writes=[TR[0]])
                    S.pool(lambda e: e.tensor_tensor(out=tmp[1], in0=x2, in1=sinv, op=ALU.mult), reads=[("qkf", sl), ("cs", g)], writes=[TR[1]])
                    S.dve(lambda e: e.tensor_tensor(out=tmp[2], in0=x2, in1=cosv, op=ALU.mult), reads=[("qkf", sl), ("cs", g)], writes=[TR[2]])
                    S.dve(lambda e: e.tensor_tensor(out=tmp[3], in0=x1, in1=sinv, op=ALU.mult), reads=[("qkf", sl), ("cs", g)], writes=[TR[3]])
                    S.pool(lambda e: e.tensor_tensor(out=ob[:, :, :, 0:16], in0=tmp[0], in1=tmp[1], op=ALU.subtract),
                           reads=[TR[0], TR[1]], writes=[("qkb", sl, "r1")])
                    S.dve(lambda e: e.tensor_tensor(out=ob[:, :, :, 16:32], in0=tmp[2], in1=tmp[3], op=ALU.add),
                          reads=[TR[2], TR[3]], writes=[("qkb", sl, "r2")])

                def T(it):
                    h, g = iters[it]
                    sl = it % 2
                    QK_RES = [("qkb", sl, "rest"), ("qkb", sl, "r1"), ("qkb", sl, "r2"), "identb"]
                    rnd = 0
                    for c, dstT in ((0, QT[sl]), (1, KT[sl])):
                        for q8 in range(2):
                            bk = 2 + rnd % 2
                            pb = bank_bf(bk)
                            for j in range(8):
                                tau = q8 * 8 + j
                                S.pe(lambda e, tau=tau, c=c, j=j, pb=pb: e.transpose(
                                    out=pb[:, j * 128:(j + 1) * 128], in_=qkb[sl][:, tau, c * 128:(c + 1) * 128], identity=identb),
                                    reads=QK_RES, writes=[("bank", bk)])
                            if rnd % 2 == 0:
                                S.act(lambda e, q8=q8, dstT=dstT, pb=pb: e.copy(out=dstT[:, q8 * 1024:(q8 + 1) * 1024], in_=pb),
                                      reads=[("bank", bk)], writes=[("QKT", sl, c)] + (["posrow"] if (it < 2 and s == 0) else []))
                            else:
                                S.dve(lambda e, q8=q8, dstT=dstT, pb=pb: e.tensor_copy(out=dstT[:, q8 * 1024:(q8 + 1) * 1024], in_=pb),
                                      reads=[("bank", bk)], writes=[("QKT", sl, c)] + (["posrow"] if (it < 2 and s == 0) else []))
                            rnd += 1

                def Att(it):
                    h, g = iters[it]
                    sl = it % 2
                    d, nb = DIL[g]
                    LA = 3

                    def Sstep(j):
                        b = j % nb
                        nq = 256 if b + 1 < nb else 128
                        sb = 6 + j % 2
                        ps_s = banks[sb]
                        S.pe(lambda e: e.matmul(ps_s[:, 0:nq], lhsT=KT[sl][:, j * 128:(j + 1) * 128],
                                                rhs=QT[sl][:, j * 128: j * 128 + nq], start=True, stop=True),
                             reads=[("QKT", sl, 0), ("QKT", sl, 1)], writes=[("bank", sb)])
                        S.act(lambda e: e.activation(out=pT[j % 6][:, 0:nq], in_=ps_s[:, 0:nq], func=AF.Exp, scale=SCALE),
                              reads=[("bank", sb)], writes=[("pT", j % 6)])
                        S.pool(lambda e: e.tensor_tensor(out=pT[j % 6][:, 0:nq], in0=pT[j % 6][:, 0:nq], in1=maskb[:, 0:nq], op=ALU.mult),
                               reads=[("pT", j % 6), "maskb"], writes=[("pT", j % 6)])

                    def PVstep(i):
                        b = i % nb
                        ob_ = 4 + (i // 2) % 2
                        col = (i % 2) * 128
                        terms = []
                        if b > 0:
                            terms.append((i - 1, pT[(i - 1) % 6][:, 128:256]))
                        terms.append((i, pT[i % 6][:, 0:128]))
                        for coff, use_v in ((0, True), (256, False)):
                            for n_, (kt, rhs) in enumerate(terms):
                                lhsT = vtok[sl][:, kt, :] if use_v else onesb
                                S.pe(lambda e, lhsT=lhsT, rhs=rhs, n_=n_, coff=coff: e.matmul(
                                    banks[ob_][:, coff + col:coff + col + 128], lhsT=lhsT, rhs=rhs, start=(n_ == 0), stop=(n_ == len(terms) - 1)),
                                    reads=[("vtok", sl), ("pT", kt % 6), "onesb"], writes=[("bank", ob_)])
                        if i % 2 == 1:
                            i2 = i // 2
                            if g == 0:
                                dn = numT[:, i2 * 256:(i2 + 1) * 256]
                                dd = denT[:, i2 * 256:(i2 + 1) * 256]
                                pn = banks[ob_][:, 0:256]
                                pd = banks[ob_][:, 256:512]
                            elif g == 1:
                                r_, b0 = divmod(i - 1, nb)
                                st_ = r_ + 4 * 128 * b0
                                dn = numT[:, st_: st_ + 255 * 4 + 1: 4]
                                dd = denT[:, st_: st_ + 255 * 4 + 1: 4]
                                pn = banks[ob_][:, 0:256]
                                pd = banks[ob_][:, 256:512]
                            else:
                                dn = numT.rearrange("p (a r) -> p r a", r=16)[:, i - 1:i + 1, :]
                                dd = denT.rearrange("p (a r) -> p r a", r=16)[:, i - 1:i + 1, :]
                                pn = banks[ob_][:, 0:256].rearrange("p (r a) -> p r a", r=2)
                                pd = banks[ob_][:, 256:512].rearrange("p (r a) -> p r a", r=2)
                            if g == 0:
                                S.act(lambda e: e.copy(out=dn, in_=pn), reads=[("bank", ob_)], writes=["numT"])
                                S.act(lambda e: e.copy(out=dd, in_=pd), reads=[("bank", ob_)], writes=["denT"])
                            else:
                                S.dve(lambda e: e.tensor_tensor(out=dn, in0=pn, in1=dn, op=ALU.add), reads=[("bank", ob_), "numT"], writes=["numT"])
                                S.dve(lambda e: e.tensor_tensor(out=dd, in0=pd, in1=dd, op=ALU.add), reads=[("bank", ob_), "denT"], writes=["denT"])

                    for step in range(16 + LA):
                        if step < 16:
                            Sstep(step)
                        if step - LA >= 0:
                            PVstep(step - LA)
                        yield
                    if g == 2:
                        def fin(c, h=h):
                            cs4 = slice(c * 512, (c + 1) * 512)
                            S.dve(lambda e: e.reciprocal(out=denT[:, cs4], in_=denT[:, cs4]), reads=["denT"], writes=["denT"])
                            S.dve(lambda e: e.tensor_tensor(out=oTb[:, cs4], in0=numT[:, cs4], in1=denT[:, cs4], op=ALU.mult), reads=["numT", "denT"], writes=["oTb"])
                            if c == 3:
                                S.dma("sp", oT_d[s, h], oTb, reads=["oTb"], writes=[("oT_d", s)], semkey="oTb")
                        for c in range(4):
                            finq.append(lambda c=c: fin(c))

                def adv(gen):
                    try:
                        next(gen)
                        return True
                    except StopIteration:
                        return False

                load_w(0)
                NI = len(iters)
                a0p = [None]
                finq = []
                for k in range(NI + 1):
                    if s == 0 and NI >= 24:
                        if a0p[0] is not None:
                            a0p[0]()
                            a0p[0] = None
                        tq = k - 4
                        if 0 <= tq + 2 < 16 and tq + 2 >= 0 and k >= 2:
                            a0_load(1, tq + 2)
                        if 0 <= tq < 16:
                            a0p[0] = a0_norm(1, tq, os.environ.get('KA_NODEFER') is None)
                    gp = P(k) if k < NI else None
                    ga = Att(k - 1) if k >= 1 else None
                    if gp is not None:
                        for _ in range(8):
                            adv(gp)
                            if finq:
                                finq.pop(0)()
                    while finq and gp is None:
                        finq.pop(0)()
                    if k >= 1:
                        T(k - 1)
                    while gp is not None or ga is not None:
                        if gp is not None and not adv(gp):
                            gp = None
                        for _ in range(3):
                            if ga is not None and not adv(ga):
                                ga = None
                while finq:
                    finq.pop(0)()

        def residual_epilogue(t, ybanks, hin_tile, hin_res, h_out_d, nxt, nxt_res, final=False, defer=False):
            sl = t % 2
            ht = hts[sl]
            htres = ("ht", sl)
            if ht is None:
                ht = hin_tile
                htres = hin_res
            for half in range(2):
                S.dve(lambda e, half=half: e.tensor_tensor(out=ht[:, half * 512:(half + 1) * 512], in0=banks[ybanks[half]][:],
                                                          in1=hin_tile[:, half * 512:(half + 1) * 512], op=ALU.add),
                      reads=[("bank", ybanks[half]), hin_res], writes=[htres])
            if not final:
                S.dma("sp", h_out_d[t * 128:(t + 1) * 128, :], ht, reads=[htres], writes=["h_out"], semkey=("hst", sl))
                return norm_transpose(ht, htres, sl, nxt, nxt_res, defer=defer)
            else:
                norm_core(ht, htres, sl, ot[sl], ("ot", sl))
                S.dma("sp", out_d[t * 128:(t + 1) * 128, :], ot[sl], reads=[("ot", sl)], semkey=("ost", sl))
                return None

        hts = [None, None]
        ot = [None, None]

        def stage_B(w_out_ap, srcT_d, gamma_ap, h_in_d, h_out_d):
            S.barrier()
            A.reset(mark0)
            xnT_all = A.alloc([8, 4096], BF16)
            wo = A.alloc([8, 1024], BF16)
            hts[0] = A.alloc([1024], F32)
            hts[1] = A.alloc([1024], F32)
            ots = [A.alloc([8, 512], BF16) for _ in range(2)]
            load_gamma(gamma_ap)
            WOB = load_cast(wo, w_out_ap.rearrange("(h d) m -> d h m", d=128), "wo")
            xin = [A.alloc([1024], F32) for _ in range(4)]
            pendB = [None]
            for t in range(32):
                s, tt = divmod(t, 16)
                sl = t % 2
                if tt % 4 == 0:
                    osl = (t // 4) % 2
                    S.dma("sp", ots[osl], srcT_d[s, :, :, tt * 128: tt * 128 + 512].rearrange("h d t -> d h t"),
                          reads=[("oT_d", s)], writes=[("ots", osl)], semkey=("ots", osl))
                if t == 0:
                    for tp in range(3):
                        S.dma("sp", xin[tp % 4], h_in_d[tp * 128:(tp + 1) * 128, :], writes=[("xin", tp % 4)], semkey=("xin", tp % 4))
                if t + 3 < 32:
                    S.dma("sp", xin[(t + 3) % 4], h_in_d[(t + 3) * 128:(t + 4) * 128, :], writes=[("xin", (t + 3) % 4)], semkey=("xin", (t + 3) % 4))
                yb = (0, 1) if sl == 0 else (2, 3)
                for half in range(2):
                    for h in range(8):
                        S.pe(lambda e, h=h, half=half, osl=osl, tt=tt, yb=yb: e.matmul(
                            banks[yb[half]][:], lhsT=ots[osl][:, h, (tt % 4) * 128:(tt % 4 + 1) * 128], rhs=wo[:, h, half * 512:(half + 1) * 512],
                            start=(h == 0), stop=(h == 7)), reads=[("ots", osl)] + WOB, writes=[("bank", yb[half])])
                p2 = residual_epilogue(t, yb, xin[t % 4], ("xin", t % 4), h_out_d, xnT_all[:, :, t * 128:(t + 1) * 128], ("xnT_all", t), defer=True)
                if pendB[0] is not None:
                    pendB[0]()
                pendB[0] = p2
            pendB[0]()
            return xnT_all

        def XR(G):
            return [("xnT_all", 4 * G + i) for i in range(4)]

        def stage_F(l, gamma_next_ap, h_in_d, h_out_d, final):
            S.barrier()
            A.reset(mark0)
            xnT_all = A.alloc([8, 4096], BF16)
            w2 = A.alloc([NFC, 1024], BF16)
            actT = A.alloc([NFC, 1024], BF16)
            w1 = [A.alloc([8, 256], BF16) for _ in range(3)]
            sg = [A.alloc([512], F32) for _ in range(2)]
            hts[0] = A.alloc([1024], F32)
            hts[1] = A.alloc([1024], F32)
            hin = [None, None]
            if final:
                ot[0] = xts[0]
                ot[1] = xts[1]
                hin[0] = A.alloc([1024], F32)
                hin[1] = A.alloc([1024], F32)
            load_gamma(gamma_next_ap)
            W2R = load_cast(w2, ffn_w_out[l].rearrange("(c f) m -> f c m", f=128), "w2", nsplit=6)
            w1v = ffn_w_in[l].rearrange("(k p) (u c f) -> p k u c f", p=128, u=2, c=NFC)
            NG = 4
            NT = NG * NFC

            def load_w1(n):
                fc = n % NFC
                sl = n % 3
                for u in range(2):
                    S.dma("poolq", w1[sl][:, :, u * 128:(u + 1) * 128], w1v[:, :, u, fc, :], writes=[("w1", sl, u)], semkey=("w1", sl, u))

            load_w1(0)
            load_w1(1)
            q = 0
            pend = [None]
            for G in range(NG):
                for fc in range(NFC):
                    n = G * NFC + fc
                    if n + 2 < NT:
                        load_w1(n + 2)
                    sl = n % 3
                    for hf in range(2):
                        pgb, pub = (0, 1) if hf == 0 else (2, 3)
                        tg = G * 2 + hf
                        for u, bk in ((0, pgb), (1, pub)):
                            for k in range(8):
                                S.pe(lambda e, u=u, bk=bk, k=k, sl=sl, tg=tg: e.matmul(
                                    banks[bk][:], lhsT=w1[sl][:, k, u * 128:(u + 1) * 128], rhs=xnT_all[:, k, tg * 512:(tg + 1) * 512],
                                    start=(k == 0), stop=(k == 7)), reads=[("w1", sl, u)] + XR(tg), writes=[("bank", bk)])
                        S.act(lambda e, hf=hf, pgb=pgb: e.activation(out=sg[hf], in_=banks[pgb][:], func=AF.Silu),
                              reads=[("bank", pgb)], writes=[("sg", hf)])
                        S.dve(lambda e, hf=hf, pub=pub, fc=fc: e.tensor_tensor(out=actT[:, fc, hf * 512:(hf + 1) * 512], in0=banks[pub][:], in1=sg[hf], op=ALU.mult),
                              reads=[("bank", pub), ("sg", hf)], writes=[("actT", fc)])
                for tt in range(8):
                    t = G * 8 + tt
                    sl = t % 2
                    if final:
                        S.dma("sp", hin[sl], h_in_d[t * 128:(t + 1) * 128, :], writes=[("hin", sl)], semkey=("hin", sl))
                        hin_t, hin_r = hin[sl], ("hin", sl)
                    else:
                        S.dma("sp", xts[sl], h_in_d[t * 128:(t + 1) * 128, :], writes=[("xt", sl)], semkey=("xt", sl))
                        hin_t, hin_r = xts[sl], ("xt", sl)
                    yb = []
                    for half in range(2):
                        bk = 4 + q % 3
                        q += 1
                        yb.append(bk)
                        for fc in range(NFC):
                            S.pe(lambda e, fc=fc, bk=bk, tt=tt, half=half: e.matmul(
                                banks[bk][:], lhsT=actT[:, fc, tt * 128:(tt + 1) * 128], rhs=w2[:, fc, half * 512:(half + 1) * 512],
                                start=(fc == 0), stop=(fc == NFC - 1)), reads=[("actT", fc), W2R[fc // 4]], writes=[("bank", bk)])
                    p2 = residual_epilogue(t, yb, hin_t, hin_r, h_out_d, xnT_all[:, :, t * 128:(t + 1) * 128], ("xnT_all", t), final=final, defer=True)
                    if pend[0] is not None:
                        pend[0]()
                    pend[0] = p2
                if pend[0] is not None:
                    pend[0]()
                    pend[0] = None

        def stage_M(gamma_next_ap, h_in_d, h_out_d):
            S.barrier()
            A.reset(mark0)
            xnT_all = A.alloc([8, 4096], BF16)
            wqk = A.alloc([8, 1024], BF16)
            wvog = A.alloc([8, 2064], BF16)
            wo = A.alloc([8, 1024], BF16)
            hts[0] = None
            hts[1] = None
            pre = A.alloc([8, 131], BF16)
            dg = A.alloc([8, 4, 128], BF16)
            qkTt = [A.alloc([8, 128], BF16) for _ in range(2)]
            vb = [A.alloc([8, 128], BF16) for _ in range(2)]
            ktok = [A.alloc([8, 64], BF16) for _ in range(2)]
            hh = [A.alloc([8, 128], F32) for _ in range(2)]
            osig = A.alloc([1024], F32)
            hgain = A.alloc([1024], F32)
            hgb = A.alloc([1024], BF16)
            hgT = A.alloc([8, 128], BF16)
            pTm = A.alloc([8, 128], BF16)
            Cst = A.alloc([4, 128], F32)
            Cb = A.alloc([4, 128], BF16)
            tmpC = A.alloc([4, 128], F32)
            cw = A.alloc([8, 4], F32)
            cb = A.alloc([8], F32)
            igb = A.alloc([8], F32)
            fgb = A.alloc([8], F32)
            onesf = A.alloc([128], F32)
            nst = A.alloc([4], F32)
            nbf = A.alloc([4], BF16)
            tmpn = A.alloc([4], F32)
            smA = [A.alloc([8, 8], F32) for _ in range(2)]
            smB = A.alloc([4, 8], F32)
            print('arena M', A.off * 4 / 1024, flush=True)

            load_gamma(gamma_next_ap)
            WQK = load_cast(wqk, mlstm_w_in[:, 0:1024].rearrange("(k p) c -> p k c", p=128), "wqk")
            WVOG = load_cast(wvog, mlstm_w_in[:, 1024:3088].rearrange("(k p) c -> p k c", p=128), "wvog")
            WO = load_cast(wo, mlstm_w_out.rearrange("(h d) m -> d h m", d=128), "wo")
            cwn = osig
            cbn = hgain
            S.dma("sp", cwn[0:4, :], conv_w, writes=["cwn"], semkey="cw")
            S.dma("sp", cbn[0:1, :], conv_b, writes=["cbn"], semkey="cb")
            for c8 in range(8):
                S.pe(lambda e, c8=c8: e.transpose(out=banks[0][:, c8 * 4:(c8 + 1) * 4], in_=cwn[0:4, c8 * 128:(c8 + 1) * 128], identity=identf[0:4, 0:4]),
                     reads=["cwn", "identf"], writes=[("bank", 0)])
                S.pe(lambda e, c8=c8: e.transpose(out=banks[0][:, 32 + c8:33 + c8], in_=cbn[0:1, c8 * 128:(c8 + 1) * 128], identity=identf[0:1, 0:1]),
                     reads=["cbn", "identf"], writes=[("bank", 0)])
            S.dve(lambda e: e.tensor_copy(out=cw, in_=banks[0][:, 0:32].rearrange("p (c j) -> p c j", c=8)), reads=[("bank", 0)], writes=["cw"])
            S.dve(lambda e: e.tensor_copy(out=cb, in_=banks[0][:, 32:40]), reads=[("bank", 0)], writes=["cb"])
            for c8 in range(8):
                for j in range(4):
                    S.dve(lambda e, c8=c8, j=j: e.tensor_scalar(out=dg[:, c8, j, :], in0=identf, scalar1=cw[:, c8, j:j + 1], scalar2=None, op0=ALU.mult),
                          reads=["cw", "identf"], writes=["dg"])
            S.dma("sp", igb, ig_bias.partition_broadcast(128), writes=["igb"], semkey="igb")
            S.dma("sp", fgb, fg_bias.partition_broadcast(128), writes=["fgb"], semkey="fgb")
            S.dma("sp", hgain, head_gain.partition_broadcast(128), writes=["cbn", "hgain"], semkey="hgain")
            S.dve(lambda e: e.memset(onesf, 1.0), writes=["onesf"])
            trif = maskf[:, 0:128]
            SM = banks[4][:, 384:512]
            LN8 = -2.0794415416798357
            B4 = ("bank", 4)
            pend = [None]

            def P1a(t):
                a = t % 2
                s, c = divmod(t, 16)
                tok = slice(t * 128, (t + 1) * 128)
                XT = [("xnT_all", t)]
                igt, fx, ee, sp = [smA[a][:, i, :] for i in range(4)]
                if c == 0:
                    S.pool(lambda e: e.memset(pre[:, :, 0:3], 0.0), writes=["pre_h"], reads=["pre"])
                else:
                    S.pool(lambda e: e.tensor_copy(out=pre[:, :, 0:3], in_=pre[:, :, 128:131]), reads=["pre"], writes=["pre_h"])
                for c8 in range(8):
                    bk = c8 // 4
                    for k in range(8):
                        S.pe(lambda e, c8=c8, k=k, bk=bk: e.matmul(banks[bk][:, (c8 % 4) * 128:(c8 % 4 + 1) * 128],
                                                                 lhsT=wqk[:, k, c8 * 128:(c8 + 1) * 128], rhs=xnT_all[:, k, tok],
                                                                 start=(k == 0), stop=(k == 7)),
                             reads=WQK + XT, writes=[("bank", bk)])
                for bk in range(2):
                    S.act(lambda e, bk=bk: e.copy(out=pre[:, bk * 4:(bk + 1) * 4, 3:131], in_=banks[bk][:].rearrange("p (c t) -> p c t", c=4)),
                          reads=[("bank", bk), "pre_h"], writes=["pre"])
                for half in range(2):
                    for k in range(8):
                        S.pe(lambda e, half=half, k=k: e.matmul(banks[2 + half][:], lhsT=xnT_all[:, k, tok], rhs=wvog[:, k, half * 512:(half + 1) * 512],
                                                               start=(k == 0), stop=(k == 7)), reads=WVOG + XT, writes=[("bank", 2 + half)])
                    S.act(lambda e, half=half: e.copy(out=vb[a][:, half * 4:(half + 1) * 4, :], in_=banks[2 + half][:].rearrange("p (h d) -> p h d", h=4)),
                          reads=[("bank", 2 + half)], writes=[("vb", a)])
                for k in range(8):
                    S.pe(lambda e, k=k: e.matmul(SM[:, 0:16], lhsT=xnT_all[:, k, tok], rhs=wvog[:, k, 2048:2064], start=(k == 0), stop=(k == 7)),
                         reads=WVOG + XT, writes=[B4])

            def P1g(t):
                a = t % 2
                igt, fx, ee, sp = [smA[a][:, i, :] for i in range(4)]
                S.dve(lambda e: e.tensor_tensor(out=igt, in0=SM[:, 0:8], in1=igb, op=ALU.add), reads=[B4, "igb"], writes=[("igt", a)])
                S.dve(lambda e: e.tensor_tensor(out=fx, in0=SM[:, 8:16], in1=fgb, op=ALU.add), reads=[B4, "fgb"], writes=[("fx", a)])
                S.act(lambda e: e.activation(out=ee, in_=fx, func=AF.Exp, scale=-1.0), reads=[("fx", a)], writes=[("ee", a)])
                S.act(lambda e: e.activation(out=sp, in_=ee, func=AF.Ln, bias=1.0), reads=[("ee", a)], writes=[("sp", a)])

            def P1b(t):
                a = t % 2
                for c8 in range(8):
                    bk = c8 // 4
                    for j in range(4):
                        S.pe(lambda e, c8=c8, j=j, bk=bk: e.matmul(banks[bk][:, (c8 % 4) * 128:(c8 % 4 + 1) * 128], lhsT=dg[:, c8, j, :], rhs=pre[:, c8, j:j + 128],
                                                                 start=(j == 0), stop=(j == 3)),
                             reads=["dg", "pre", "pre_h"], writes=[("bank", bk)])
                for c8 in range(8):
                    bk = c8 // 4
                    S.act(lambda e, c8=c8, bk=bk: e.activation(out=qkTt[a][:, c8, :], in_=banks[bk][:, (c8 % 4) * 128:(c8 % 4 + 1) * 128], func=AF.Silu, bias=cb[:, c8:c8 + 1]),
                          reads=[("bank", bk), "cb"], writes=[("qkTt", a, c8)])

            def P1c(t):
                a = t % 2
                igt, fx, ee, sp, tsum, cs_, zf, dl = [smA[a][:, i, :] for i in range(8)]
                S.pe(lambda e: e.matmul(SM[:, 16:24], lhsT=trif, rhs=sp, start=True, stop=True), reads=[("sp", a), "maskf"], writes=[B4])
                S.pe(lambda e: e.matmul(SM[:, 24:32], lhsT=onesf, rhs=sp, start=True, stop=True), reads=[("sp", a), "onesf"], writes=[B4])
                S.dve(lambda e: e.tensor_tensor(out=tsum, in0=SM[:, 16:24], in1=igt, op=ALU.add), reads=[B4, ("igt", a)], writes=[("tsum", a)])
                S.act(lambda e: e.activation(out=zf, in_=SM[:, 16:24], func=AF.Exp), reads=[B4], writes=[("zf", a)])
                S.act(lambda e: e.activation(out=dl, in_=SM[:, 24:32], func=AF.Exp, scale=-1.0), reads=[B4], writes=[("dl", a)])
                S.act(lambda e: e.activation(out=cs_, in_=tsum, func=AF.Exp, bias=LN8), reads=[("tsum", a)], writes=[("cs_", a)])
                pb7 = bank_bf(7)
                for c4 in range(4):
                    S.pe(lambda e, c4=c4: e.transpose(out=pb7[:, c4 * 128:(c4 + 1) * 128], in_=qkTt[a][:, 4 + c4, :], identity=identb),
                         reads=[("qkTt", a, 4 + c4), "identb"], writes=[("bank", 7)])
                S.dve(lambda e: e.tensor_tensor(out=ktok[a], in0=pb7[:, 0:512].rearrange("p (h d) -> p h d", h=8),
                                                in1=cs_.unsqueeze(2).to_broadcast([128, 8, 64]), op=ALU.mult),
                      reads=[("bank", 7), ("cs_", a)], writes=[("ktok", a)])

            def P2(t):
                a = t % 2
                s, c = divmod(t, 16)
                cs_, zf, dl = [smA[a][:, i, :] for i in (5, 6, 7)]
                zz, rz = smB[:, 0, :], smB[:, 1, :]
                if c == 0:
                    S.dve(lambda e: e.memset(Cst, 0.0), writes=[("Cst", 0), ("Cst", 1)])
                    S.dve(lambda e: e.memset(Cb, 0.0), writes=[("Cb", 0), ("Cb", 1)])
                    S.dve(lambda e: e.memset(nst, 0.0), writes=[("nst", 0), ("nst", 1)])
                    S.dve(lambda e: e.memset(nbf, 0.0), writes=[("nbf", 0), ("nbf", 1)])

                def half_body(hp):
                    R = slice(hp * 64, (hp + 1) * 64)
                    def st_mm(j):
                        S.pe(lambda e, j=j: e.matmul(banks[4][:, (j % 3) * 128:(j % 3 + 1) * 128], lhsT=qkTt[a][R, 4 + j, :], rhs=qkTt[a][R, j, :], start=True, stop=True),
                             reads=[("qkTt", a, 4 + j), ("qkTt", a, j)], writes=[B4])
                    for j in range(3):
                        st_mm(j)
                    for j in range(4):
                        h = 2 * j + hp
                        S.dve(lambda e, j=j, h=h: e.scalar_tensor_tensor(out=pTm[:, h, :], in0=banks[4][:, (j % 3) * 128:(j % 3 + 1) * 128],
                                                                        scalar=cs_[:, h:h + 1], in1=trif, op0=ALU.mult, op1=ALU.mult),
                              reads=[B4, ("cs_", a), "maskf"], writes=[("pTm", h)])
                        if j == 0:
                            st_mm(3)
                        S.pe(lambda e, j=j, h=h: e.matmul(banks[5][:, j * 128:(j + 1) * 128], lhsT=pTm[:, h, :], rhs=vb[a][:, h, :], start=True, stop=False),
                             reads=[("pTm", h), ("vb", a)], writes=[("bank", 5)])
                        S.pe(lambda e, j=j, h=h: e.matmul(banks[5][:, j * 128:(j + 1) * 128], lhsT=qkTt[a][R, j, :], rhs=Cb[R, j, :], start=False, stop=True),
                             reads=[("qkTt", a, j), ("Cb", hp)], writes=[("bank", 5)])
                    for j in range(4):
                        h = 2 * j + hp
                        S.pe(lambda e, j=j, h=h: e.matmul(SM[:, 32 + h:33 + h], lhsT=pTm[:, h, :], rhs=onesb[:, 0:1], start=True, stop=False),
                             reads=[("pTm", h), "onesb"], writes=[B4])
                        S.pe(lambda e, j=j, h=h: e.matmul(SM[:, 32 + h:33 + h], lhsT=qkTt[a][R, j, :], rhs=nbf[R, j:j + 1], start=False, stop=True),
                             reads=[("qkTt", a, j), ("nbf", hp)], writes=[B4])
                    for j in range(4):
                        h = 2 * j + hp
                        S.pe(lambda e, j=j, h=h: e.matmul(banks[6][:, j * 128:(j + 1) * 128], lhsT=ktok[a][:, j * 2:j * 2 + 2, :].rearrange("p a d -> p (a d)"), rhs=vb[a][:, h, :], start=True, stop=True),
                             reads=[("ktok", a), ("vb", a)], writes=[("bank", 6)])
                    if hp == 0:
                        for j in range(4):
                            S.pe(lambda e, j=j: e.matmul(SM[:, 40 + j:41 + j], lhsT=ktok[a][:, j * 2:j * 2 + 2, :].rearrange("p a d -> p (a d)"), rhs=onesb[:, 0:1], start=True, stop=True),
                                 reads=[("ktok", a), "onesb"], writes=[B4])
                    dlv = dl[R, hp:8:2]
                    S.dve(lambda e: e.tensor_tensor(out=tmpC[R], in0=banks[6][R, :].rearrange("p (j d) -> p j d", j=4), in1=Cst[R], op=ALU.add),
                          reads=[("bank", 6), ("Cst", hp)], writes=[("tmpC", hp)])
                    S.dve(lambda e: e.tensor_tensor(out=Cst[R], in0=tmpC[R], in1=dlv.unsqueeze(2).to_broadcast([64, 4, 128]), op=ALU.mult),
                          reads=[("tmpC", hp), ("dl", a)], writes=[("Cst", hp)])
                    S.act(lambda e: e.copy(out=Cb[R], in_=Cst[R]), reads=[("Cst", hp)], writes=[("Cb", hp)])
                    S.dve(lambda e: e.tensor_tensor(out=tmpn[R], in0=SM[R, 40:44], in1=nst[R], op=ALU.add), reads=[B4, ("nst", hp)], writes=[("tmpn", hp)])
                    S.dve(lambda e: e.tensor_tensor(out=nst[R], in0=tmpn[R], in1=dlv, op=ALU.mult), reads=[("tmpn", hp), ("dl", a)], writes=[("nst", hp)])
                    S.pool(lambda e: e.tensor_copy(out=nbf[R], in_=nst[R]), reads=[("nst", hp)], writes=[("nbf", hp)])
                    dv_ = SM[:, 32 + hp:40:2]
                    zv = zz[:, hp:8:2]
                    rzv = rz[:, hp:8:2]
                    S.act(lambda e: e.activation(out=zv, in_=dv_, func=AF.Abs), reads=[B4], writes=[("zz", hp)])
                    S.dve(lambda e: e.tensor_tensor(out=zv, in0=zv, in1=zf[:, hp:8:2], op=ALU.max), reads=[("zz", hp), ("zf", a)], writes=[("zz", hp)])
                    S.dve(lambda e: e.reciprocal(out=rzv, in_=zv), reads=[("zz", hp)], writes=[("rz", hp)])
                    S.dve(lambda e: e.tensor_tensor(out=hh[a][:, hp:8:2, :], in0=banks[5][:].rearrange("p (j d) -> p j d", j=4),
                                                    in1=rzv.unsqueeze(2).to_broadcast([128, 4, 128]), op=ALU.mult),
                          reads=[("bank", 5), ("rz", hp)], writes=[("hh", a, hp)])
                return half_body

            def P3sq(t):
                a = t % 2
                ss8 = smB[:, 2, :]
                S.dma("sp", xts[t % 2], h_in_d[t * 128:(t + 1) * 128, :], writes=[("xt", t % 2)], semkey=("xt", t % 2))
                for h in range(8):
                    S.act(lambda e, h=h: e.activation(out=hgb[:, h * 128:(h + 1) * 128], in_=hh[a][:, h, :], func=AF.Square, accum_out=ss8[:, h:h + 1]),
                          reads=[("hh", a, h % 2), "hgbfree"], writes=[("ss8", h)])

            def P3n(t):
                a = t % 2
                ss8, rs8 = smB[:, 2, :], smB[:, 3, :]
                HH = [("hh", a, 0), ("hh", a, 1)]
                S.pool(lambda e: e.tensor_scalar(out=ss8, in0=ss8, scalar1=1.0 / 128, scalar2=1e-6, op0=ALU.mult, op1=ALU.add),
                       reads=[("ss8", h) for h in range(8)], writes=["ss8v"])
                S.pool(lambda e: e.tensor_tensor(out=rs8, in0=ss8, in1=mhalf[:, 0:8], op=ALU.pow), reads=["ss8v", "mhalf"], writes=["rs8"])
                S.dve(lambda e: e.tensor_tensor(out=hh[a], in0=hh[a], in1=rs8.unsqueeze(2).to_broadcast([128, 8, 128]), op=ALU.mult), reads=HH + ["rs8"], writes=HH)

            def P3o(t):
                tok = slice(t * 128, (t + 1) * 128)
                XT = [("xnT_all", t)]
                for half in range(2):
                    for k in range(8):
                        S.pe(lambda e, half=half, k=k: e.matmul(banks[2 + half][:], lhsT=xnT_all[:, k, tok], rhs=wvog[:, k, 1024 + half * 512:1024 + (half + 1) * 512],
                                                               start=(k == 0), stop=(k == 7)), reads=WVOG + XT, writes=[("bank", 2 + half)])
                    S.act(lambda e, half=half: e.activation(out=osig[:, half * 512:(half + 1) * 512], in_=banks[2 + half][:], func=AF.Sigmoid),
                          reads=[("bank", 2 + half)], writes=[("osig", half)] + (["cwn"] if t == 0 else []))
                S.pool(lambda e: e.tensor_tensor(out=osig, in0=osig, in1=hgain, op=ALU.mult), reads=[("osig", 0), ("osig", 1), "hgain"], writes=["og"])

            def P3h(t):
                a = t % 2
                HH = [("hh", a, 0), ("hh", a, 1)]
                S.dve(lambda e: e.tensor_tensor(out=hgb, in0=hh[a].rearrange("p h d -> p (h d)"), in1=osig, op=ALU.mult),
                      reads=HH + ["og"] + [("ss8", h) for h in range(8)], writes=["hgb"])

            def P3b(t):
                a = t % 2
                sl = t % 2
                tok = slice(t * 128, (t + 1) * 128)
                pb7 = bank_bf(7)
                if pend[0] is not None:
                    pend[0]()
                    pend[0] = None
                for ch in range(8):
                    S.pe(lambda e, ch=ch: e.transpose(out=pb7[:, ch * 128:(ch + 1) * 128], in_=hgb[:, ch * 128:(ch + 1) * 128], identity=identb),
                         reads=["hgb", "identb"], writes=[("bank", 7)])
                S.act(lambda e: e.copy(out=hgT, in_=pb7.rearrange("p (k t) -> p k t", k=8)), reads=[("bank", 7), "hgb"], writes=["hgT", "hgbfree"])
                for half in range(2):
                    for ch in range(8):
                        S.pe(lambda e, half=half, ch=ch: e.matmul(banks[half][:], lhsT=hgT[:, ch, :], rhs=wo[:, ch, half * 512:(half + 1) * 512],
                                                                 start=(ch == 0), stop=(ch == 7)), reads=["hgT"] + WO, writes=[("bank", half)])
                pend[0] = residual_epilogue(t, (0, 1), xts[sl], ("xt", sl), h_out_d, xnT_all[:, :, tok], ("xnT_all", t), defer=True)

            print('arena M2', A.off * 4 / 1024, flush=True)

            NTL = 32
            P1a(0)
            P1b(0)
            P1g(0)
            P1c(0)
            for i in range(NTL + 1):
                if 1 <= i:
                    P3sq(i - 1)
                if i + 1 < NTL:
                    P1a(i + 1)
                if 1 <= i:
                    P3n(i - 1)
                    P3o(i - 1)
                if i + 1 < NTL:
                    P1b(i + 1)
                    P1g(i + 1)
                if i < NTL:
                    hb = P2(i)
                    hb(0)
                    hb(1)
                if 1 <= i:
                    P3h(i - 1)
                    P3b(i - 1)
                if i + 1 < NTL:
                    P1c(i + 1)
            if pend[0] is not None:
                pend[0]()
                pend[0] = None


        stage_A()
        if n_stage >= 2:
            stage_B(attn_w_out, oT_d, ffn_norm[0:1, :], x_d, hA_d)
        if n_stage >= 3:
            stage_F(0, mlstm_norm, hA_d, hB_d, False)
        if n_stage >= 4:
            stage_M(ffn_norm[1:2, :], hB_d, hA_d)
        if n_stage >= 5:
            stage_F(1, final_norm, hA_d, None, True)
        S.emit()
    return nc


def _consts():
    k = np.arange(128)[:, None]
    q = np.arange(128)[None, :]
    mask2 = np.concatenate([(q >= k), (k >= q)], axis=1).astype(np.float32)
    invf = (np.float32(500000.0) ** (-(np.arange(16, dtype=np.float32) * np.float32(2.0) / np.float32(32.0)))).astype(np.float32)
    return {"ident": np.eye(128, dtype=np.float32), "mask2": mask2,
            "invf": np.ascontiguousarray(np.broadcast_to(invf[None, :], (128, 16))).astype(np.float32)}


def make_in_maps(inputs, cores):
    f = lambda a: np.ascontiguousarray(np.asarray(a))
    shared = {
        "attn_norm": f(inputs["attn_norm"]).reshape(1, 1024),
        "attn_w_in": f(inputs["attn_w_in"]).reshape(1024, 9216),
        "attn_w_out": f(inputs["attn_w_out"]).reshape(1024, 1024),
        "mlstm_norm": f(inputs["mlstm_norm"]).reshape(1, 1024),
        "mlstm_w_in": f(inputs["mlstm_w_in"]).reshape(1024, 3088),
        "conv_w": f(inputs["mlstm_conv_w"]).reshape(4, 1024),
        "conv_b": f(inputs["mlstm_conv_b"]).reshape(1, 1024),
        "ig_bias": f(inputs["mlstm_ig_bias"]).reshape(1, 8),
        "fg_bias": f(inputs["mlstm_fg_bias"]).reshape(1, 8),
        "head_gain": f(inputs["mlstm_head_gain"]).reshape(1, 1024),
        "mlstm_w_out": f(inputs["mlstm_w_out"]).reshape(1024, 1024),
        "ffn_norm": f(inputs["ffn_norm"]).reshape(2, 1024),
        "ffn_w_in": f(inputs["ffn_w_in"]).reshape(2, 1024, 2 * DFF),
        "ffn_w_out": f(inputs["ffn_w_out"]).reshape(2, DFF, 1024),
        "final_norm": f(inputs["final_norm"]).reshape(1, 1024),
    }
    shared.update(_consts())
    x = f(inputs["x"])
    pos = f(inputs["positions"]).astype(np.int32)
    maps = []
    for c in cores:
        m = dict(shared)
        m["x"] = np.ascontiguousarray(x[2 * c:2 * c + 2].reshape(4096, 1024))
        m["pos"] = np.ascontiguousarray(pos[2 * c:2 * c + 2])
        maps.append(m)
    return maps


def kernel(**inputs):
    nc = build_program()
    in_maps = make_in_maps(inputs, list(range(8)))
    res = run_bass_kernel_spmd(nc, in_maps, core_ids=list(range(8)))
    outs = [np.asarray(r["out"]).reshape(2, 2048, 1024) for r in res.results]
    return np.concatenate(outs, axis=0).astype(np.float32)
```

```python
import contextlib
import os
import numpy as np
import concourse.bass as bass
import concourse.mybir as mybir
from concourse.bass_utils import run_bass_kernel_spmd

F32 = mybir.dt.float32
BF16 = mybir.dt.bfloat16
I32 = mybir.dt.int32
AF = mybir.ActivationFunctionType
ALU = mybir.AluOpType
AX = mybir.AxisListType

COMPUTE = ("pe", "act", "dve", "pool")
DMAQ = ("sp", "actq", "poolq")
STREAM_OF = {"pe": "pe", "act": "act", "dve": "dve", "pool": "pool", "sp": "sp", "actq": "act", "poolq": "pool"}


class _Op:
    __slots__ = ("eng", "fn", "deps", "dma", "semkey", "signal", "count", "idx")


class Sched:
    def __init__(self, nc):
        self.nc = nc
        self.ops = []
        self.res_w = {}
        self.res_r = {}

    def add(self, eng, fn, reads=(), writes=(), semkey=None):
        idx = len(self.ops)
        deps = set()
        bl = self.__dict__.setdefault("bank_last", {})
        for r in list(reads) + list(writes):
            if isinstance(r, tuple) and r[0] == "bank":
                last = bl.setdefault(r, {})
                for e2, i2 in last.items():
                    if STREAM_OF[e2] != STREAM_OF[eng]:
                        deps.add(i2)
                last[eng] = idx
        reads = [r for r in reads if not (isinstance(r, tuple) and r[0] == "bank")]
        writes = [r for r in writes if not (isinstance(r, tuple) and r[0] == "bank")]
        for r in reads:
            w = self.res_w.get(r)
            if w is not None:
                deps.add(w)
        for wr in writes:
            w = self.res_w.get(wr)
            if w is not None:
                deps.add(w)
            lastrd = {}
            for rd in self.res_r.get(wr, ()):
                rop = self.ops[rd]
                if rop.dma:
                    deps.add(rd)
                else:
                    lastrd[rop.eng] = rd
            deps.update(lastrd.values())
        for r in reads:
            self.res_r.setdefault(r, []).append(idx)
        for wr in writes:
            self.res_w[wr] = idx
            self.res_r[wr] = []
        op = _Op()
        op.eng = eng
        op.fn = fn
        op.dma = eng in DMAQ
        op.semkey = semkey if op.dma else None
        if op.dma:
            assert semkey is not None
        op.deps = deps
        op.signal = op.dma
        op.count = 0
        op.idx = idx
        self.ops.append(op)
        return idx

    def pe(self, fn, reads=(), writes=()):
        return self.add("pe", fn, reads, writes)

    def act(self, fn, reads=(), writes=()):
        return self.add("act", fn, reads, writes)

    def dve(self, fn, reads=(), writes=()):
        return self.add("dve", fn, reads, writes)

    def pool(self, fn, reads=(), writes=()):
        return self.add("pool", fn, reads, writes)

    def dma(self, q, out, in_, reads=(), writes=(), semkey=None, **kw):
        def fn(e, out=out, in_=in_, kw=kw):
            return e.dma_start(out=out, in_=in_, **kw)
        return self.add(q, fn, reads, writes, semkey=semkey)

    def emit(self, final_wait_stream="sp"):
        nc = self.nc
        ops = self.ops
        needed = []
        for op in ops:
            nd = []
            for d in op.deps:
                dop = ops[d]
                same_stream = STREAM_OF[dop.eng] == STREAM_OF[op.eng]
                if dop.dma:
                    nd.append(d)
                elif same_stream:
                    if dop.eng != "pe":
                        nd.append(d)
                else:
                    nd.append(d)
            needed.append(nd)
            for d in nd:
                ops[d].signal = True
        cnt = {}
        for op in ops:
            if op.signal:
                key = ("dma", op.semkey) if op.dma else ("eng", op.eng)
                inc = 16 if op.dma else 1
                cnt[key] = cnt.get(key, 0) + inc
                op.count = cnt[key]
        keys = list(cnt.keys())
        stack = contextlib.ExitStack()
        sems = {}
        with stack:
            for k in keys:
                sems[k] = stack.enter_context(nc.semaphore("s_%s_%s" % (k[0], str(k[1]).replace(" ", ""))))
            block = stack.enter_context(nc.Block())
            streams = {"pe": [], "act": [], "dve": [], "pool": [], "sp": []}
            for op in ops:
                streams[STREAM_OF[op.eng]].append(op)

            def build(stream_name, eng):
                waited = {}
                for op in streams[stream_name]:
                    for d in needed[op.idx]:
                        dop = ops[d]
                        key = ("dma", dop.semkey) if dop.dma else ("eng", dop.eng)
                        if waited.get(key, 0) >= dop.count:
                            continue
                        eng.wait_ge(sems[key], dop.count)
                        waited[key] = dop.count
                    ins = op.fn(eng)
                    if op.signal:
                        key = ("dma", op.semkey) if op.dma else ("eng", op.eng)
                        ins.then_inc(sems[key], 16 if op.dma else 1)
                if stream_name == final_wait_stream:
                    for k in keys:
                        if k[0] == "dma":
                            eng.wait_ge(sems[k], cnt[k])

            @block.tensor
            def _(e):
                build("pe", e)

            @block.scalar
            def _(e):
                build("act", e)

            @block.vector
            def _(e):
                build("dve", e)

            @block.gpsimd
            def _(e):
                build("pool", e)

            @block.sync
            def _(e):
                build("sp", e)


def _sched_barrier(self):
    last = getattr(self, "_bar_last", {})
    dmas = []
    for op in self.ops[getattr(self, "_bar_start", 0):]:
        if op.dma:
            dmas.append(op.idx)
        else:
            last[op.eng] = op.idx
    self._bar_last = last
    self._bar_start = len(self.ops)
    self._bar_deps = set(last.values()) | set(dmas)
    self._bar_seen = set()


_orig_add = Sched.add


def _add_with_barrier(self, eng, fn, reads=(), writes=(), semkey=None):
    idx = _orig_add(self, eng, fn, reads, writes, semkey)
    bd = getattr(self, "_bar_deps", None)
    if bd:
        st = STREAM_OF[eng]
        if st not in self._bar_seen:
            self._bar_seen.add(st)
            self.ops[idx].deps |= bd
    return idx


Sched.add = _add_with_barrier
Sched.barrier = _sched_barrier


class Arena:
    def __init__(self, ap_f32, nwords):
        self.base = ap_f32
        self.n = nwords
        self.off = 0

    def mark(self):
        return self.off

    def reset(self, m):
        self.off = m

    def alloc(self, free_shape, dt):
        n = 1
        for v in free_shape:
            n *= v
        words = (n * (2 if dt == BF16 else 4) + 3) // 4
        words = (words + 7) // 8 * 8
        assert self.off + words <= self.n, ("SBUF arena overflow", self.off, words, self.n)
        v = self.base[:, self.off:self.off + words]
        self.off += words
        if dt == BF16:
            v = v.bitcast(BF16)[:, 0:n]
        elif dt == I32:
            v = v.bitcast(I32)[:, 0:n]
        else:
            v = v[:, 0:n]
        if len(free_shape) == 2:
            v = v.rearrange("p (a b) -> p a b", a=free_shape[0])
        elif len(free_shape) == 3:
            v = v.rearrange("p (a b c) -> p a b c", a=free_shape[0], b=free_shape[1])
        return v


DIL = ((1, 16), (4, 4), (16, 1))
TWO_PI = 6.283185307179586
C1 = 6.28125
C2 = TWO_PI - C1
PI = 3.141592653589793
DFF = 2816
NFC = 22


def build_program(n_stage=99, dbg=None):
    nc = bass.Bass("TRN2", target_bir_lowering=False)

    def din(name, shape, dt=F32):
        return nc.dram_tensor(name, shape, dt, kind="ExternalInput").ap()

    x_d = din("x", [4096, 1024])
    pos_d = din("pos", [2, 2048], I32)
    attn_norm = din("attn_norm", [1, 1024])
    attn_w_in = din("attn_w_in", [1024, 9216])
    attn_w_out = din("attn_w_out", [1024, 1024])
    mlstm_norm = din("mlstm_norm", [1, 1024])
    mlstm_w_in = din("mlstm_w_in", [1024, 3088])
    conv_w = din("conv_w", [4, 1024])
    conv_b = din("conv_b", [1, 1024])
    ig_bias = din("ig_bias", [1, 8])
    fg_bias = din("fg_bias", [1, 8])
    head_gain = din("head_gain", [1, 1024])
    mlstm_w_out = din("mlstm_w_out", [1024, 1024])
    ffn_norm = din("ffn_norm", [2, 1024])
    ffn_w_in = din("ffn_w_in", [2, 1024, 2 * DFF])
    ffn_w_out = din("ffn_w_out", [2, DFF, 1024])
    final_norm = din("final_norm", [1, 1024])
    ident_d = din("ident", [128, 128])
    mask_d = din("mask2", [128, 256])
    invf_d = din("invf", [128, 16])

    out_d = nc.dram_tensor("out", [4096, 1024], F32, kind="ExternalOutput").ap()
    dkind = "ExternalOutput" if dbg else "Internal"
    oT_d = nc.dram_tensor("oT_d", [2, 8, 128, 2048], BF16, kind=dkind).ap()
    hA_d = nc.dram_tensor("hA_d", [4096, 1024], F32, kind=dkind).ap()
    hB_d = nc.dram_tensor("hB_d", [4096, 1024], F32, kind=dkind).ap()

    S = Sched(nc)
    NW = 52224
    stack = contextlib.ExitStack()
    with stack:
        arena_t = stack.enter_context(nc.sbuf_tensor("arena", [128, NW], F32))
        banks = [stack.enter_context(nc.psum_tensor("bank%d" % i, [128, 512], F32)) for i in range(8)]
        A = Arena(arena_t[:], NW)

        def bank_bf(i):
            return banks[i][:].bitcast(BF16)

        identf = A.alloc([128], F32)
        identb = A.alloc([128], BF16)
        maskf = A.alloc([256], F32)
        maskb = A.alloc([256], BF16)
        onesb = A.alloc([128], BF16)
        gbc = A.alloc([1024], F32)
        xts = [A.alloc([1024], F32) for _ in range(2)]
        xnb = [A.alloc([1024], BF16) for _ in range(2)]
        stat = A.alloc([2, 8], F32)
        S.dma("sp", identf, ident_d, writes=["identf"], semkey="c_ident")
        S.dma("sp", maskf, mask_d, writes=["maskf"], semkey="c_mask")
        S.dve(lambda e: e.tensor_copy(out=identb, in_=identf), reads=["identf"], writes=["identb"])
        S.dve(lambda e: e.tensor_copy(out=maskb, in_=maskf), reads=["maskf"], writes=["maskb"])
        S.dve(lambda e: e.memset(onesb, 1.0), writes=["onesb"])
        mark0 = A.mark()


        def load_cast(dst, src, name, nsplit=4):
            n = dst.shape[1]
            step = (n + nsplit - 1) // nsplit
            res = []
            for i, lo in enumerate(range(0, n, step)):
                hi = min(n, lo + step)
                S.dma("poolq", dst[:, lo:hi], src[:, lo:hi], writes=[(name, i)], semkey=(name, i))
                res.append((name, i))
            return res

        def load_gamma(g_ap):
            S.dma("sp", gbc, g_ap.partition_broadcast(128), writes=["gbc"], semkey="gbc")

        def norm_core(src, src_res, slot, out_ap, out_res):
            ss = stat[:, slot, 0:1]
            vv = stat[:, slot, 1:2]
            rs = stat[:, slot, 2:3]
            sres = ("stat", slot)
            S.act(lambda e: e.activation(out=xnb[slot], in_=src, func=AF.Square, accum_out=ss),
                  reads=[src_res], writes=[sres, ("xnb", slot)])
            S.dve(lambda e: e.tensor_scalar(out=vv, in0=ss, scalar1=1.0 / 1024, scalar2=1e-6, op0=ALU.mult, op1=ALU.add),
                  reads=[sres], writes=[(sres, "v")])
            S.pool(lambda e: e.tensor_tensor(out=rs, in0=vv, in1=mhalf[:, 0:1], op=ALU.pow),
                   reads=[(sres, "v"), "mhalf"], writes=[(sres, "r")])
            S.dve(lambda e: e.scalar_tensor_tensor(out=out_ap, in0=src, scalar=rs, in1=gbc, op0=ALU.mult, op1=ALU.mult),
                  reads=[src_res, (sres, "r"), "gbc"], writes=[out_res])

        mhalf = A.alloc([8], F32)
        S.dve(lambda e: e.memset(mhalf, -0.5), writes=["mhalf"])
        mark0 = A.mark()
        TB = 7

        def norm_transpose(src, src_res, slot, dstT, dst_res, defer=False):
            xb = xnb[slot]
            norm_core(src, src_res, slot, xb, ("xnb", slot))

            def part2():
                pt = bank_bf(TB)
                for k in range(8):
                    S.pe(lambda e, k=k: e.transpose(out=pt[:, k * 128:(k + 1) * 128], in_=xb[:, k * 128:(k + 1) * 128], identity=identb),
                         reads=[("xnb", slot), "identb"], writes=[("bank", TB)])
                S.act(lambda e: e.copy(out=dstT, in_=pt.rearrange("p (k t) -> p k t", k=8)),
                      reads=[("bank", TB)], writes=[dst_res])
            if defer:
                return part2
            part2()
            return None

        def stage_A():
            A.reset(mark0)
            xnTs = [A.alloc([8, 2048], BF16) for _ in range(2)]
            xa = [A.alloc([1024], F32) for _ in range(4)]
            numT = A.alloc([2048], F32)
            denT = A.alloc([2048], F32)
            oTb = A.alloc([2048], BF16)
            wt = [A.alloc([8, 384], BF16) for _ in range(2)]
            cs = [A.alloc([2, 32, 16], F32) for _ in range(3)]
            vtok = [A.alloc([16, 128], BF16) for _ in range(2)]
            qkb = [A.alloc([16, 256], BF16) for _ in range(2)]
            qkf = [A.alloc([16, 2, 32], F32) for _ in range(2)]
            qk_region = A.alloc([8192], BF16)
            QT = [qk_region[:, 0:2048], qk_region[:, 2048:4096]]
            KT = [qk_region[:, 4096:6144], qk_region[:, 6144:8192]]
            tmp = [A.alloc([16, 2, 16], F32) for _ in range(4)]
            pT = [A.alloc([256], BF16) for _ in range(6)]
            posi = A.alloc([32], I32)
            posf = A.alloc([32], F32)
            invf = A.alloc([16], F32)
            def _v(ap, lo, dt):
                v = ap[:, lo:lo + 512]
                if dt == I32:
                    v = v.bitcast(I32)
                return v.rearrange("p (a b) -> p a b", a=32)
            ang = _v(xa[0], 0, F32)
            rr = _v(xa[0], 512, F32)
            r2 = _v(xa[1], 0, F32)
            nf = _v(xa[1], 512, F32)
            ni = _v(xa[2], 0, I32)
            mk = _v(xa[2], 512, F32)

            print('arena A', A.off * 4 / 1024, flush=True)
            load_gamma(attn_norm)
            S.dma("sp", invf, invf_d, writes=["invf"], semkey="c_invf")
            posrow = qk_region.bitcast(F32)
            posrow_i = qk_region.bitcast(I32)
            onef1 = A.alloc([8], F32)
            S.dma("sp", posrow_i[0:1, :], pos_d.rearrange("(o s) t -> o (s t)", o=1), writes=["posrow_i"], semkey="c_pos")
            S.dve(lambda e: e.tensor_copy(out=posrow[0:1, :], in_=posrow_i[0:1, :]), reads=["posrow_i"], writes=["posrow"])
            S.dve(lambda e: e.memset(onef1, 1.0), writes=["onef1"])
            for g, (d, nb) in enumerate(DIL[:int(os.environ.get('KA_ROPE', '3'))]):
                for s in range(2):
                    for tau in range(16):
                        r_, b_ = divmod(tau, nb)
                        base = s * 2048 + 128 * b_ * d + r_
                        S.pe(lambda e, s=s, tau=tau, base=base, d=d: e.matmul(
                            banks[0][:, s * 16 + tau: s * 16 + tau + 1], lhsT=posrow[0:1, base: base + 127 * d + 1: d], rhs=onef1[0:1, 0:1],
                            start=True, stop=True), reads=["posrow", "onef1"], writes=[("bank", 0)])
                S.dve(lambda e: e.tensor_copy(out=posf, in_=banks[0][:, 0:32]), reads=[("bank", 0)], writes=["posf"])
                R = ["rope_tmp", ("xa", 0), ("xa", 1), ("xa", 2)]
                S.dve(lambda e: e.tensor_tensor(out=ang, in0=posf.unsqueeze(2).to_broadcast([128, 32, 16]),
                                                in1=invf.unsqueeze(1).to_broadcast([128, 32, 16]), op=ALU.mult),
                      reads=R + ["invf", "posf"], writes=R)
                S.dve(lambda e: e.tensor_scalar(out=nf, in0=ang, scalar1=1.0 / TWO_PI, scalar2=None, op0=ALU.mult), reads=R, writes=R)
                S.dve(lambda e: e.tensor_copy(out=ni, in_=nf), reads=R, writes=R)
                S.dve(lambda e: e.tensor_copy(out=nf, in_=ni), reads=R, writes=R)
                S.dve(lambda e: e.scalar_tensor_tensor(out=rr, in0=nf, scalar=-C1, in1=ang, op0=ALU.mult, op1=ALU.add), reads=R, writes=R)
                S.dve(lambda e: e.scalar_tensor_tensor(out=rr, in0=nf, scalar=-C2, in1=rr, op0=ALU.mult, op1=ALU.add), reads=R, writes=R)

                def wrap(t):
                    S.dve(lambda e: e.tensor_single_scalar(out=mk, in_=t, scalar=PI, op=ALU.is_gt), reads=R, writes=R)
                    S.dve(lambda e: e.scalar_tensor_tensor(out=t, in0=mk, scalar=-TWO_PI, in1=t, op0=ALU.mult, op1=ALU.add), reads=R, writes=R)
                    S.dve(lambda e: e.tensor_single_scalar(out=mk, in_=t, scalar=-PI, op=ALU.is_lt), reads=R, writes=R)
                    S.dve(lambda e: e.scalar_tensor_tensor(out=t, in0=mk, scalar=TWO_PI, in1=t, op0=ALU.mult, op1=ALU.add), reads=R, writes=R)
                    S.dve(lambda e: e.tensor_scalar(out=t, in0=t, scalar1=3.1415925, scalar2=-3.1415925, op0=ALU.min, op1=ALU.max), reads=R, writes=R)
                wrap(rr)
                S.dve(lambda e: e.tensor_scalar(out=r2, in0=rr, scalar1=PI / 2, scalar2=None, op0=ALU.add), reads=R, writes=R)
                wrap(r2)
                S.act(lambda e, g=g: e.activation(out=cs[g][:, 1], in_=rr, func=AF.Sin), reads=R, writes=[("cs", g)])
                S.act(lambda e, g=g: e.activation(out=cs[g][:, 0], in_=r2, func=AF.Sin), reads=R, writes=[("cs", g)])

            w_in_v = attn_w_in.rearrange("(k p) (g t h d) -> p k g t h d", p=128, g=3, t=3, h=8)
            SCALE = 128.0 ** -0.5
            iters = [(h, g) for h in range(8) for g in range(3)][:int(os.environ.get('KA_ITERS', '24'))]
            KA_LEVEL = int(os.environ.get('KA_LEVEL', '9'))

            def load_w(it):
                h, g = iters[it]
                sl = it % 2
                for t3 in range(3):
                    S.dma("poolq", wt[sl][:, :, t3 * 128:(t3 + 1) * 128], w_in_v[:, :, g, t3, h, :],
                          writes=[("wt", sl, t3)], semkey=("wt", sl, t3))

            def tok_slice(g, tau, n_tiles=1):
                d, nb = DIL[g]
                r, b = divmod(tau, nb)
                base = 128 * b * d + r
                return slice(base, base + (128 * n_tiles - 1) * d + 1, d)

            def a0_load(sq, t):
                gt = sq * 16 + t
                S.dma("sp", xa[gt % 4], x_d[gt * 128:(gt + 1) * 128, :], writes=[("xa", gt % 4)], semkey=("xa", gt % 4))

            def a0_norm(sq, t, defer):
                gt = sq * 16 + t
                return norm_transpose(xa[gt % 4], ("xa", gt % 4), gt % 2, xnTs[sq][:, :, t * 128:(t + 1) * 128], ("xnT", sq), defer=defer)

            for t in range(3):
                a0_load(0, t)
            for t in range(16):
                if t + 3 < 16:
                    a0_load(0, t + 3)
                a0_norm(0, t, False)

            for s in range(2):
                xnT = xnTs[s]
                XN = ("xnT", s)

                def P(it):
                    h, g = iters[it]
                    sl = it % 2
                    if it + 1 < len(iters):
                        load_w(it + 1)
                    for tau in range(16):
                        pb = tau % 2
                        pp = banks[pb]
                        ts_ = tok_slice(g, tau)
                        for k in range(8):
                            S.pe(lambda e, k=k, pp=pp, ts_=ts_, sl=sl, xnT=xnT: e.matmul(pp[:, 0:384], lhsT=xnT[:, k, ts_], rhs=wt[sl][:, k, :],
                                                                              start=(k == 0), stop=(k == 7)),
                                 reads=[XN, ("wt", sl, 0), ("wt", sl, 1), ("wt", sl, 2)], writes=[("bank", pb)])
                        ppv = pp[:, 0:256].rearrange("p (c e) -> p c e", c=2)
                        o_rest = qkb[sl][:, tau, :].rearrange("p (c e) -> p c e", c=2)[:, :, 32:128]
                        if tau < 6 or tau % 2 == 0:
                            S.act(lambda e, pp=pp, tau=tau: e.copy(out=vtok[sl][:, tau, :], in_=pp[:, 256:384]),
                                  reads=[("bank", pb)], writes=[("vtok", sl)])
                            S.act(lambda e, ppv=ppv, o_rest=o_rest: e.copy(out=o_rest, in_=ppv[:, :, 32:128]),
                                  reads=[("bank", pb)], writes=[("qkb", sl, "rest")])
                            S.act(lambda e, ppv=ppv, tau=tau: e.copy(out=qkf[sl][:, tau, :, :], in_=ppv[:, :, 0:32]),
                                  reads=[("bank", pb)], writes=[("qkf", sl)])
                        else:
                            S.dve(lambda e, pp=pp, tau=tau: e.tensor_copy(out=vtok[sl][:, tau, :], in_=pp[:, 256:384]),
                                  reads=[("bank", pb)], writes=[("vtok", sl)])
                            S.dve(lambda e, ppv=ppv, o_rest=o_rest: e.tensor_copy(out=o_rest, in_=ppv[:, :, 32:128]),
                                  reads=[("bank", pb)], writes=[("qkb", sl, "rest")])
                            S.dve(lambda e, ppv=ppv, tau=tau: e.tensor_copy(out=qkf[sl][:, tau, :, :], in_=ppv[:, :, 0:32]),
                                  reads=[("bank", pb)], writes=[("qkf", sl)])
                        yield
                    cosv = cs[g][:, 0, s * 16:(s + 1) * 16, :].unsqueeze(2).to_broadcast([128, 16, 2, 16])
                    sinv = cs[g][:, 1, s * 16:(s + 1) * 16, :].unsqueeze(2).to_broadcast([128, 16, 2, 16])
                    x1 = qkf[sl][:, :, :, 0:16]
                    x2 = qkf[sl][:, :, :, 16:32]
                    ob = qkb[sl].rearrange("p t (c e) -> p t c e", c=2)
                    TR = [("ropet", i) for i in range(4)]
                    S.pool(lambda e: e.tensor_tensor(out=tmp[0], in0=x1, in1=cosv, op=ALU.mult), reads=[("qkf", sl), ("cs", g)], writes=[TR[0]])
                    S.pool(lambda e: e.tensor_tensor(out=tmp[1], in0=x2, in1=sinv, op=ALU.mult), reads=[("qkf", sl), ("cs", g)], writes=[TR[1]])
                    S.dve(lambda e: e.tensor_tensor(out=tmp[2], in0=x2, in1=cosv, op=ALU.mult), reads=[("qkf", sl), ("cs", g)], writes=[TR[2]])
                    S.dve(lambda e: e.tensor_tensor(out=tmp[3], in0=x1, in1=sinv, op=ALU.mult), reads=[("qkf", sl), ("cs", g)], writes=[TR[3]])
                    S.pool(lambda e: e.tensor_tensor(out=ob[:, :, :, 0:16], in0=tmp[0], in1=tmp[1], op=ALU.subtract),
                           reads=[TR[0], TR[1]], writes=[("qkb", sl, "r1")])
                    S.dve(lambda e: e.tensor_tensor(out=ob[:, :, :, 16:32], in0=tmp[2], in1=tmp[3], op=ALU.add),
                          reads=[TR[2], TR[3]], writes=[("qkb", sl, "r2")])

                def T(it):
                    h, g = iters[it]
                    sl = it % 2
                    QK_RES = [("qkb", sl, "rest"), ("qkb", sl, "r1"), ("qkb", sl, "r2"), "identb"]
                    rnd = 0
                    for c, dstT in ((0, QT[sl]), (1, KT[sl])):
                        for q8 in range(2):
                            bk = 2 + rnd % 2
                            pb = bank_bf(bk)
                            for j in range(8):
                                tau = q8 * 8 + j
                                S.pe(lambda e, tau=tau, c=c, j=j, pb=pb: e.transpose(
                                    out=pb[:, j * 128:(j + 1) * 128], in_=qkb[sl][:, tau, c * 128:(c + 1) * 128], identity=identb),
                                    reads=QK_RES, writes=[("bank", bk)])
                            if rnd % 2 == 0:
                                S.act(lambda e, q8=q8, dstT=dstT, pb=pb: e.copy(out=dstT[:, q8 * 1024:(q8 + 1) * 1024], in_=pb),
                                      reads=[("bank", bk)], writes=[("QKT", sl, c)] + (["posrow"] if (it < 2 and s == 0) else []))
                            else:
                                S.dve(lambda e, q8=q8, dstT=dstT, pb=pb: e.tensor_copy(out=dstT[:, q8 * 1024:(q8 + 1) * 1024], in_=pb),
                                      reads=[("bank", bk)], writes=[("QKT", sl, c)] + (["posrow"] if (it < 2 and s == 0) else []))
                            rnd += 1

                def Att(it):
                    h, g = iters[it]
                    sl = it % 2
                    d, nb = DIL[g]
                    LA = 3

                    def Sstep(j):
                        b = j % nb
                        nq = 256 if b + 1 < nb else 128
                        sb = 6 + j % 2
                        ps_s = banks[sb]
                        S.pe(lambda e: e.matmul(ps_s[:, 0:nq], lhsT=KT[sl][:, j * 128:(j + 1) * 128],
                                                rhs=QT[sl][:, j * 128: j * 128 + nq], start=True, stop=True),
                             reads=[("QKT", sl, 0), ("QKT", sl, 1)], writes=[("bank", sb)])
                        S.act(lambda e: e.activation(out=pT[j % 6][:, 0:nq], in_=ps_s[:, 0:nq], func=AF.Exp, scale=SCALE),
                              reads=[("bank", sb)], writes=[("pT", j % 6)])
                        S.pool(lambda e: e.tensor_tensor(out=pT[j % 6][:, 0:nq], in0=pT[j % 6][:, 0:nq], in1=maskb[:, 0:nq], op=ALU.mult),
                               reads=[("pT", j % 6), "maskb"], writes=[("pT", j % 6)])

                    def PVstep(i):
                        b = i % nb
                        ob_ = 4 + (i // 2) % 2
                        col = (i % 2) * 128
                        terms = []
                        if b > 0:
                            terms.append((i - 1, pT[(i - 1) % 6][:, 128:256]))
                        terms.append((i, pT[i % 6][:, 0:128]))
                        for coff, use_v in ((0, True), (256, False)):
                            for n_, (kt, rhs) in enumerate(terms):
                                lhsT = vtok[sl][:, kt, :] if use_v else onesb
                                S.pe(lambda e, lhsT=lhsT, rhs=rhs, n_=n_, coff=coff: e.matmul(
                                    banks[ob_][:, coff + col:coff + col + 128], lhsT=lhsT, rhs=rhs, start=(n_ == 0), stop=(n_ == len(terms) - 1)),
                                    reads=[("vtok", sl), ("pT", kt % 6), "onesb"], writes=[("bank", ob_)])
                        if i % 2 == 1:
                            i2 = i // 2
                            if g == 0:
                                dn = numT[:, i2 * 256:(i2 + 1) * 256]
                                dd = denT[:, i2 * 256:(i2 + 1) * 256]
                                pn = banks[ob_][:, 0:256]
                                pd = banks[ob_][:, 256:512]
                            elif g == 1:
                                r_, b0 = divmod(i - 1, nb)
                                st_ = r_ + 4 * 128 * b0
                                dn = numT[:, st_: st_ + 255 * 4 + 1: 4]
                                dd = denT[:, st_: st_ + 255 * 4 + 1: 4]
                                pn = banks[ob_][:, 0:256]
                                pd = banks[ob_][:, 256:512]
                            else:
                                dn = numT.rearrange("p (a r) -> p r a", r=16)[:, i - 1:i + 1, :]
                                dd = denT.rearrange("p (a r) -> p r a", r=16)[:, i - 1:i + 1, :]
                                pn = banks[ob_][:, 0:256].rearrange("p (r a) -> p r a", r=2)
                                pd = banks[ob_][:, 256:512].rearrange("p (r a) -> p r a", r=2)
                            if g == 0:
                                S.act(lambda e: e.copy(out=dn, in_=pn), reads=[("bank", ob_)], writes=["numT"])
                                S.act(lambda e: e.copy(out=dd, in_=pd), reads=[("bank", ob_)], writes=["denT"])
                            else:
                                S.dve(lambda e: e.tensor_tensor(out=dn, in0=pn, in1=dn, op=ALU.add), reads=[("bank", ob_), "numT"], writes=["numT"])
                                S.dve(lambda e: e.tensor_tensor(out=dd, in0=pd, in1=dd, op=ALU.add), reads=[("bank", ob_), "denT"], writes=["denT"])

                    for step in range(16 + LA):
                        if step < 16:
                            Sstep(step)
                        if step - LA >= 0:
                            PVstep(step - LA)
                        yield
                    if g == 2:
                        def fin(c, h=h):
                            cs4 = slice(c * 512, (c + 1) * 512)
                            S.dve(lambda e: e.reciprocal(out=denT[:, cs4], in_=denT[:, cs4]), reads=["denT"], writes=["denT"])
                            S.dve(lambda e: e.tensor_tensor(out=oTb[:, cs4], in0=numT[:, cs4], in1=denT[:, cs4], op=ALU.mult), reads=["numT", "denT"], writes=["oTb"])
                            if c == 3:
                                S.dma("sp", oT_d[s, h], oTb, reads=["oTb"], writes=[("oT_d", s)], semkey="oTb")
                        for c in range(4):
                            finq.append(lambda c=c: fin(c))

                def adv(gen):
                    try:
                        next(gen)
                        return True
                    except StopIteration:
                        return False

                load_w(0)
                NI = len(iters)
                a0p = [None]
                finq = []
                for k in range(NI + 1):
                    if s == 0 and NI >= 24:
                        if a0p[0] is not None:
                            a0p[0]()
                            a0p[0] = None
                        tq = k - 4
                        if 0 <= tq + 2 < 16 and tq + 2 >= 0 and k >= 2:
                            a0_load(1, tq + 2)
                        if 0 <= tq < 16:
                            a0p[0] = a0_norm(1, tq, os.environ.get('KA_NODEFER') is None)
                    gp = P(k) if k < NI else None
                    ga = Att(k - 1) if k >= 1 else None
                    if gp is not None:
                        for _ in range(8):
                            adv(gp)
                            if finq:
                                finq.pop(0)()
                    while finq and gp is None:
                        finq.pop(0)()
                    if k >= 1:
                        T(k - 1)
                    while gp is not None or ga is not None:
                        if gp is not None and not adv(gp):
                            gp = None
                        for _ in range(3):
                            if ga is not None and not adv(ga):
                                ga = None
                while finq:
                    finq.pop(0)()

        def residual_epilogue(t, ybanks, hin_tile, hin_res, h_out_d, nxt, nxt_res, final=False, defer=False):
            sl = t % 2
            ht = hts[sl]
            htres = ("ht", sl)
            if ht is None:
                ht = hin_tile
                htres = hin_res
            for half in range(2):
                S.dve(lambda e, half=half: e.tensor_tensor(out=ht[:, half * 512:(half + 1) * 512], in0=banks[ybanks[half]][:],
                                                          in1=hin_tile[:, half * 512:(half + 1) * 512], op=ALU.add),
                      reads=[("bank", ybanks[half]), hin_res], writes=[htres])
            if not final:
                S.dma("sp", h_out_d[t * 128:(t + 1) * 128, :], ht, reads=[htres], writes=["h_out"], semkey=("hst", sl))
                return norm_transpose(ht, htres, sl, nxt, nxt_res, defer=defer)
            else:
                norm_core(ht, htres, sl, ot[sl], ("ot", sl))
                S.dma("sp", out_d[t * 128:(t + 1) * 128, :], ot[sl], reads=[("ot", sl)], semkey=("ost", sl))
                return None

        hts = [None, None]
        ot = [None, None]

        def stage_B(w_out_ap, srcT_d, gamma_ap, h_in_d, h_out_d):
            S.barrier()
            A.reset(mark0)
            xnT_all = A.alloc([8, 4096], BF16)
            wo = A.alloc([8, 1024], BF16)
            hts[0] = A.alloc([1024], F32)
            hts[1] = A.alloc([1024], F32)
            ots = [A.alloc([8, 512], BF16) for _ in range(2)]
            load_gamma(gamma_ap)
            WOB = load_cast(wo, w_out_ap.rearrange("(h d) m -> d h m", d=128), "wo")
            xin = [A.alloc([1024], F32) for _ in range(4)]
            pendB = [None]
            for t in range(32):
                s, tt = divmod(t, 16)
                sl = t % 2
                if tt % 4 == 0:
                    osl = (t // 4) % 2
                    S.dma("sp", ots[osl], srcT_d[s, :, :, tt * 128: tt * 128 + 512].rearrange("h d t -> d h t"),
                          reads=[("oT_d", s)], writes=[("ots", osl)], semkey=("ots", osl))
                if t == 0:
                    for tp in range(3):
                        S.dma("sp", xin[tp % 4], h_in_d[tp * 128:(tp + 1) * 128, :], writes=[("xin", tp % 4)], semkey=("xin", tp % 4))
                if t + 3 < 32:
                    S.dma("sp", xin[(t + 3) % 4], h_in_d[(t + 3) * 128:(t + 4) * 128, :], writes=[("xin", (t + 3) % 4)], semkey=("xin", (t + 3) % 4))
                yb = (0, 1) if sl == 0 else (2, 3)
                for half in range(2):
                    for h in range(8):
                        S.pe(lambda e, h=h, half=half, osl=osl, tt=tt, yb=yb: e.matmul(
                            banks[yb[half]][:], lhsT=ots[osl][:, h, (tt % 4) * 128:(tt % 4 + 1) * 128], rhs=wo[:, h, half * 512:(half + 1) * 512],
                            start=(h == 0), stop=(h == 7)), reads=[("ots", osl)] + WOB, writes=[("bank", yb[half])])
                p2 = residual_epilogue(t, yb, xin[t % 4], ("xin", t % 4), h_out_d, xnT_all[:, :, t * 128:(t + 1) * 128], ("xnT_all", t), defer=True)
                if pendB[0] is not None:
                    pendB[0]()
                pendB[0] = p2
            pendB[0]()
            return xnT_all

        def XR(G):
            return [("xnT_all", 4 * G + i) for i in range(4)]

        def stage_F(l, gamma_next_ap, h_in_d, h_out_d, final):
            S.barrier()
            A.reset(mark0)
            xnT_all = A.alloc([8, 4096], BF16)
            w2 = A.alloc([NFC, 1024], BF16)
            actT = A.alloc([NFC, 1024], BF16)
            w1 = [A.alloc([8, 256], BF16) for _ in range(3)]
            sg = [A.alloc([512], F32) for _ in range(2)]
            hts[0] = A.alloc([1024], F32)
            hts[1] = A.alloc([1024], F32)
            hin = [None, None]
            if final:
                ot[0] = xts[0]
                ot[1] = xts[1]
                hin[0] = A.alloc([1024], F32)
                hin[1] = A.alloc([1024], F32)
            load_gamma(gamma_next_ap)
            W2R = load_cast(w2, ffn_w_out[l].rearrange("(c f) m -> f c m", f=128), "w2", nsplit=6)
            w1v = ffn_w_in[l].rearrange("(k p) (u c f) -> p k u c f", p=128, u=2, c=NFC)
            NG = 4
            NT = NG * NFC

            def load_w1(n):
                fc = n % NFC
                sl = n % 3
                for u in range(2):
                    S.dma("poolq", w1[sl][:, :, u * 128:(u + 1) * 128], w1v[:, :, u, fc, :], writes=[("w1", sl, u)], semkey=("w1", sl, u))

            load_w1(0)
            load_w1(1)
            q = 0
            pend = [None]
            for G in range(NG):
                for fc in range(NFC):
                    n = G * NFC + fc
                    if n + 2 < NT:
                        load_w1(n + 2)
                    sl = n % 3
                    for hf in range(2):
                        pgb, pub = (0, 1) if hf == 0 else (2, 3)
                        tg = G * 2 + hf
                        for u, bk in ((0, pgb), (1, pub)):
                            for k in range(8):
                                S.pe(lambda e, u=u, bk=bk, k=k, sl=sl, tg=tg: e.matmul(
                                    banks[bk][:], lhsT=w1[sl][:, k, u * 128:(u + 1) * 128], rhs=xnT_all[:, k, tg * 512:(tg + 1) * 512],
                                    start=(k == 0), stop=(k == 7)), reads=[("w1", sl, u)] + XR(tg), writes=[("bank", bk)])
                        S.act(lambda e, hf=hf, pgb=pgb: e.activation(out=sg[hf], in_=banks[pgb][:], func=AF.Silu),
                              reads=[("bank", pgb)], writes=[("sg", hf)])
                        S.dve(lambda e, hf=hf, pub=pub, fc=fc: e.tensor_tensor(out=actT[:, fc, hf * 512:(hf + 1) * 512], in0=banks[pub][:], in1=sg[hf], op=ALU.mult),
                              reads=[("bank", pub), ("sg", hf)], writes=[("actT", fc)])
                for tt in range(8):
                    t = G * 8 + tt
                    sl = t % 2
                    if final:
                        S.dma("sp", hin[sl], h_in_d[t * 128:(t + 1) * 128, :], writes=[("hin", sl)], semkey=("hin", sl))
                        hin_t, hin_r = hin[sl], ("hin", sl)
                    else:
                        S.dma("sp", xts[sl], h_in_d[t * 128:(t + 1) * 128, :], writes=[("xt", sl)], semkey=("xt", sl))
                        hin_t, hin_r = xts[sl], ("xt", sl)
                    yb = []
                    for half in range(2):
                        bk = 4 + q % 3
                        q += 1
                        yb.append(bk)
                        for fc in range(NFC):
                            S.pe(lambda e, fc=fc, bk=bk, tt=tt, half=half: e.matmul(
                                banks[bk][:], lhsT=actT[:, fc, tt * 128:(tt + 1) * 128], rhs=w2[:, fc, half * 512:(half + 1) * 512],
                                start=(fc == 0), stop=(fc == NFC - 1)), reads=[("actT", fc), W2R[fc // 4]], writes=[("bank", bk)])
                    p2 = residual_epilogue(t, yb, hin_t, hin_r, h_out_d, xnT_all[:, :, t * 128:(t + 1) * 128], ("xnT_all", t), final=final, defer=True)
                    if pend[0] is not None:
                        pend[0]()
                    pend[0] = p2
                if pend[0] is not None:
                    pend[0]()
                    pend[0] = None

        def stage_M(gamma_next_ap, h_in_d, h_out_d):
            S.barrier()
            A.reset(mark0)
            xnT_all = A.alloc([8, 4096], BF16)
            wqk = A.alloc([8, 1024], BF16)
            wvog = A.alloc([8, 2064], BF16)
            wo = A.alloc([8, 1024], BF16)
            hts[0] = None
            hts[1] = None
            pre = A.alloc([8, 131], BF16)
            dg = A.alloc([8, 4, 128], BF16)
            qkTt = [A.alloc([8, 128], BF16) for _ in range(2)]
            vb = [A.alloc([8, 128], BF16) for _ in range(2)]
            ktok = [A.alloc([8, 64], BF16) for _ in range(2)]
            hh = [A.alloc([8, 128], F32) for _ in range(2)]
            osig = A.alloc([1024], F32)
            hgain = A.alloc([1024], F32)
            hgb = A.alloc([1024], BF16)
            hgT = A.alloc([8, 128], BF16)
            pTm = A.alloc([8, 128], BF16)
            Cst = A.alloc([4, 128], F32)
            Cb = A.alloc([4, 128], BF16)
            tmpC = A.alloc([4, 128], F32)
            cw = A.alloc([8, 4], F32)
            cb = A.alloc([8], F32)
            igb = A.alloc([8], F32)
            fgb = A.alloc([8], F32)
            onesf = A.alloc([128], F32)
            nst = A.alloc([4], F32)
            nbf = A.alloc([4], BF16)
            tmpn = A.alloc([4], F32)
            smA = [A.alloc([8, 8], F32) for _ in range(2)]
            smB = A.alloc([4, 8], F32)
            print('arena M', A.off * 4 / 1024, flush=True)

            load_gamma(gamma_next_ap)
            WQK = load_cast(wqk, mlstm_w_in[:, 0:1024].rearrange("(k p) c -> p k c", p=128), "wqk")
            WVOG = load_cast(wvog, mlstm_w_in[:, 1024:3088].rearrange("(k p) c -> p k c", p=128), "wvog")
            WO = load_cast(wo, mlstm_w_out.rearrange("(h d) m -> d h m", d=128), "wo")
            cwn = osig
            cbn = hgain
            S.dma("sp", cwn[0:4, :], conv_w, writes=["cwn"], semkey="cw")
            S.dma("sp", cbn[0:1, :], conv_b, writes=["cbn"], semkey="cb")
            for c8 in range(8):
                S.pe(lambda e, c8=c8: e.transpose(out=banks[0][:, c8 * 4:(c8 + 1) * 4], in_=cwn[0:4, c8 * 128:(c8 + 1) * 128], identity=identf[0:4, 0:4]),
                     reads=["cwn", "identf"], writes=[("bank", 0)])
                S.pe(lambda e, c8=c8: e.transpose(out=banks[0][:, 32 + c8:33 + c8], in_=cbn[0:1, c8 * 128:(c8 + 1) * 128], identity=identf[0:1, 0:1]),
                     reads=["cbn", "identf"], writes=[("bank", 0)])
            S.dve(lambda e: e.tensor_copy(out=cw, in_=banks[0][:, 0:32].rearrange("p (c j) -> p c j", c=8)), reads=[("bank", 0)], writes=["cw"])
            S.dve(lambda e: e.tensor_copy(out=cb, in_=banks[0][:, 32:40]), reads=[("bank", 0)], writes=["cb"])
            for c8 in range(8):
                for j in range(4):
                    S.dve(lambda e, c8=c8, j=j: e.tensor_scalar(out=dg[:, c8, j, :], in0=identf, scalar1=cw[:, c8, j:j + 1], scalar2=None, op0=ALU.mult),
                          reads=["cw", "identf"], writes=["dg"])
            S.dma("sp", igb, ig_bias.partition_broadcast(128), writes=["igb"], semkey="igb")
            S.dma("sp", fgb, fg_bias.partition_broadcast(128), writes=["fgb"], semkey="fgb")
            S.dma("sp", hgain, head_gain.partition_broadcast(128), writes=["cbn", "hgain"], semkey="hgain")
            S.dve(lambda e: e.memset(onesf, 1.0), writes=["onesf"])
            trif = maskf[:, 0:128]
            SM = banks[4][:, 384:512]
            LN8 = -2.0794415416798357
            B4 = ("bank", 4)
            pend = [None]

            def P1a(t):
                a = t % 2
                s, c = divmod(t, 16)
                tok = slice(t * 128, (t + 1) * 128)
                XT = [("xnT_all", t)]
                igt, fx, ee, sp = [smA[a][:, i, :] for i in range(4)]
                if c == 0:
                    S.pool(lambda e: e.memset(pre[:, :, 0:3], 0.0), writes=["pre_h"], reads=["pre"])
                else:
                    S.pool(lambda e: e.tensor_copy(out=pre[:, :, 0:3], in_=pre[:, :, 128:131]), reads=["pre"], writes=["pre_h"])
                for c8 in range(8):
                    bk = c8 // 4
                    for k in range(8):
                        S.pe(lambda e, c8=c8, k=k, bk=bk: e.matmul(banks[bk][:, (c8 % 4) * 128:(c8 % 4 + 1) * 128],
                                                                 lhsT=wqk[:, k, c8 * 128:(c8 + 1) * 128], rhs=xnT_all[:, k, tok],
                                                                 start=(k == 0), stop=(k == 7)),
                             reads=WQK + XT, writes=[("bank", bk)])
                for bk in range(2):
                    S.act(lambda e, bk=bk: e.copy(out=pre[:, bk * 4:(bk + 1) * 4, 3:131], in_=banks[bk][:].rearrange("p (c t) -> p c t", c=4)),
                          reads=[("bank", bk), "pre_h"], writes=["pre"])
                for half in range(2):
                    for k in range(8):
                        S.pe(lambda e, half=half, k=k: e.matmul(banks[2 + half][:], lhsT=xnT_all[:, k, tok], rhs=wvog[:, k, half * 512:(half + 1) * 512],
                                                               start=(k == 0), stop=(k == 7)), reads=WVOG + XT, writes=[("bank", 2 + half)])
                    S.act(lambda e, half=half: e.copy(out=vb[a][:, half * 4:(half + 1) * 4, :], in_=banks[2 + half][:].rearrange("p (h d) -> p h d", h=4)),
                          reads=[("bank", 2 + half)], writes=[("vb", a)])
                for k in range(8):
                    S.pe(lambda e, k=k: e.matmul(SM[:, 0:16], lhsT=xnT_all[:, k, tok], rhs=wvog[:, k, 2048:2064], start=(k == 0), stop=(k == 7)),
                         reads=WVOG + XT, writes=[B4])

            def P1g(t):
                a = t % 2
                igt, fx, ee, sp = [smA[a][:, i, :] for i in range(4)]
                S.dve(lambda e: e.tensor_tensor(out=igt, in0=SM[:, 0:8], in1=igb, op=ALU.add), reads=[B4, "igb"], writes=[("igt", a)])
                S.dve(lambda e: e.tensor_tensor(out=fx, in0=SM[:, 8:16], in1=fgb, op=ALU.add), reads=[B4, "fgb"], writes=[("fx", a)])
                S.act(lambda e: e.activation(out=ee, in_=fx, func=AF.Exp, scale=-1.0), reads=[("fx", a)], writes=[("ee", a)])
                S.act(lambda e: e.activation(out=sp, in_=ee, func=AF.Ln, bias=1.0), reads=[("ee", a)], writes=[("sp", a)])

            def P1b(t):
                a = t % 2
                for c8 in range(8):
                    bk = c8 // 4
                    for j in range(4):
                        S.pe(lambda e, c8=c8, j=j, bk=bk: e.matmul(banks[bk][:, (c8 % 4) * 128:(c8 % 4 + 1) * 128], lhsT=dg[:, c8, j, :], rhs=pre[:, c8, j:j + 128],
                                                                 start=(j == 0), stop=(j == 3)),
                             reads=["dg", "pre", "pre_h"], writes=[("bank", bk)])
                for c8 in range(8):
                    bk = c8 // 4
                    S.act(lambda e, c8=c8, bk=bk: e.activation(out=qkTt[a][:, c8, :], in_=banks[bk][:, (c8 % 4) * 128:(c8 % 4 + 1) * 128], func=AF.Silu, bias=cb[:, c8:c8 + 1]),
                          reads=[("bank", bk), "cb"], writes=[("qkTt", a, c8)])

            def P1c(t):
                a = t % 2
                igt, fx, ee, sp, tsum, cs_, zf, dl = [smA[a][:, i, :] for i in range(8)]
                S.pe(lambda e: e.matmul(SM[:, 16:24], lhsT=trif, rhs=sp, start=True, stop=True), reads=[("sp", a), "maskf"], writes=[B4])
                S.pe(lambda e: e.matmul(SM[:, 24:32], lhsT=onesf, rhs=sp, start=True, stop=True), reads=[("sp", a), "onesf"], writes=[B4])
                S.dve(lambda e: e.tensor_tensor(out=tsum, in0=SM[:, 16:24], in1=igt, op=ALU.add), reads=[B4, ("igt", a)], writes=[("tsum", a)])
                S.act(lambda e: e.activation(out=zf, in_=SM[:, 16:24], func=AF.Exp), reads=[B4], writes=[("zf", a)])
                S.act(lambda e: e.activation(out=dl, in_=SM[:, 24:32], func=AF.Exp, scale=-1.0), reads=[B4], writes=[("dl", a)])
                S.act(lambda e: e.activation(out=cs_, in_=tsum, func=AF.Exp, bias=LN8), reads=[("tsum", a)], writes=[("cs_", a)])
                pb7 = bank_bf(7)
                for c4 in range(4):
                    S.pe(lambda e, c4=c4: e.transpose(out=pb7[:, c4 * 128:(c4 + 1) * 128], in_=qkTt[a][:, 4 + c4, :], identity=identb),
                         reads=[("qkTt", a, 4 + c4), "identb"], writes=[("bank", 7)])
                S.dve(lambda e: e.tensor_tensor(out=ktok[a], in0=pb7[:, 0:512].rearrange("p (h d) -> p h d", h=8),
                                                in1=cs_.unsqueeze(2).to_broadcast([128, 8, 64]), op=ALU.mult),
                      reads=[("bank", 7), ("cs_", a)], writes=[("ktok", a)])

            def P2(t):
                a = t % 2
                s, c = divmod(t, 16)
                cs_, zf, dl = [smA[a][:, i, :] for i in (5, 6, 7)]
                zz, rz = smB[:, 0, :], smB[:, 1, :]
                if c == 0:
                    S.dve(lambda e: e.memset(Cst, 0.0), writes=[("Cst", 0), ("Cst", 1)])
                    S.dve(lambda e: e.memset(Cb, 0.0), writes=[("Cb", 0), ("Cb", 1)])
                    S.dve(lambda e: e.memset(nst, 0.0), writes=[("nst", 0), ("nst", 1)])
                    S.dve(lambda e: e.memset(nbf, 0.0), writes=[("nbf", 0), ("nbf", 1)])

                def half_body(hp):
                    R = slice(hp * 64, (hp + 1) * 64)
                    def st_mm(j):
                        S.pe(lambda e, j=j: e.matmul(banks[4][:, (j % 3) * 128:(j % 3 + 1) * 128], lhsT=qkTt[a][R, 4 + j, :], rhs=qkTt[a][R, j, :], start=True, stop=True),
                             reads=[("qkTt", a, 4 + j), ("qkTt", a, j)], writes=[B4])
                    for j in range(3):
                        st_mm(j)
                    for j in range(4):
                        h = 2 * j + hp
                        S.dve(lambda e, j=j, h=h: e.scalar_tensor_tensor(out=pTm[:, h, :], in0=banks[4][:, (j % 3) * 128:(j % 3 + 1) * 128],
                                                                        scalar=cs_[:, h:h + 1], in1=trif, op0=ALU.mult, op1=ALU.mult),
                              reads=[B4, ("cs_", a), "maskf"], writes=[("pTm", h)])
                        if j == 0:
                            st_mm(3)
                        S.pe(lambda e, j=j, h=h: e.matmul(banks[5][:, j * 128:(j + 1) * 128], lhsT=pTm[:, h, :], rhs=vb[a][:, h, :], start=True, stop=False),
                             reads=[("pTm", h), ("vb", a)], writes=[("bank", 5)])
                        S.pe(lambda e, j=j, h=h: e.matmul(banks[5][:, j * 128:(j + 1) * 128], lhsT=qkTt[a][R, j, :], rhs=Cb[R, j, :], start=False, stop=True),
                             reads=[("qkTt", a, j), ("Cb", hp)], writes=[("bank", 5)])
                    for j in range(4):
                        h = 2 * j + hp
                        S.pe(lambda e, j=j, h=h: e.matmul(SM[:, 32 + h:33 + h], lhsT=pTm[:, h, :], rhs=onesb[:, 0:1], start=True, stop=False),
                             reads=[("pTm", h), "onesb"], writes=[B4])
                        S.pe(lambda e, j=j, h=h: e.matmul(SM[:, 32 + h:33 + h], lhsT=qkTt[a][R, j, :], rhs=nbf[R, j:j + 1], start=False, stop=True),
                             reads=[("qkTt", a, j), ("nbf", hp)], writes=[B4])
                    for j in range(4):
                        h = 2 * j + hp
                        S.pe(lambda e, j=j, h=h: e.matmul(banks[6][:, j * 128:(j + 1) * 128], lhsT=ktok[a][:, j * 2:j * 2 + 2, :].rearrange("p a d -> p (a d)"), rhs=vb[a][:, h, :], start=True, stop=True),
                             reads=[("ktok", a), ("vb", a)], writes=[("bank", 6)])
                    if hp == 0:
                        for j in range(4):
                            S.pe(lambda e, j=j: e.matmul(SM[:, 40 + j:41 + j], lhsT=ktok[a][:, j * 2:j * 2 + 2, :].rearrange("p a d -> p (a d)"), rhs=onesb[:, 0:1], start=True, stop=True),
                                 reads=[("ktok", a), "onesb"], writes=[B4])
                    dlv = dl[R, hp:8:2]
                    dv_ = SM[:, 32 + hp:40:2]
                    zv = zz[:, hp:8:2]
                    rzv = rz[:, hp:8:2]
                    S.dve(lambda e: e.scalar_tensor_tensor(out=zv, in0=dv_, scalar=-1.0, in1=zf[:, hp:8:2], op0=ALU.mult, op1=ALU.max), reads=[B4, ("zf", a)], writes=[("zz", hp)])
                    S.dve(lambda e: e.tensor_tensor(out=zv, in0=dv_, in1=zv, op=ALU.max), reads=[B4, ("zz", hp)], writes=[("zz", hp)])
                    S.dve(lambda e: e.tensor_tensor(out=tmpn[R], in0=SM[R, 40:44], in1=nst[R], op=ALU.add), reads=[B4, ("nst", hp)], writes=[("tmpn", hp)])
                    S.dve(lambda e: e.tensor_tensor(out=tmpC[R], in0=banks[6][R, :].rearrange("p (j d) -> p j d", j=4), in1=Cst[R], op=ALU.add),
                          reads=[("bank", 6), ("Cst", hp)], writes=[("tmpC", hp)])
                    S.dve(lambda e: e.reciprocal(out=rzv, in_=zv), reads=[("zz", hp)], writes=[("rz", hp)])
                    S.dve(lambda e: e.tensor_tensor(out=hh[a][:, hp:8:2, :], in0=banks[5][:].rearrange("p (j d) -> p j d", j=4),
                                                    in1=rzv.unsqueeze(2).to_broadcast([128, 4, 128]), op=ALU.mult),
                          reads=[("bank", 5), ("rz", hp)], writes=[("hh", a, hp)])
                    S.dve(lambda e: e.tensor_tensor(out=Cst[R], in0=tmpC[R], in1=dlv.unsqueeze(2).to_broadcast([64, 4, 128]), op=ALU.mult),
                          reads=[("tmpC", hp), ("dl", a)], writes=[("Cst", hp)])
                    S.act(lambda e: e.copy(out=Cb[R], in_=Cst[R]), reads=[("Cst", hp)], writes=[("Cb", hp)])
                    S.dve(lambda e: e.tensor_tensor(out=nst[R], in0=tmpn[R], in1=dlv, op=ALU.mult), reads=[("tmpn", hp), ("dl", a)], writes=[("nst", hp)])
                    S.pool(lambda e: e.tensor_copy(out=nbf[R], in_=nst[R]), reads=[("nst", hp)], writes=[("nbf", hp)])
                return half_body

            def P3sq(t):
                a = t % 2
                ss8 = smB[:, 2, :]
                S.dma("sp", xts[t % 2], h_in_d[t * 128:(t + 1) * 128, :], writes=[("xt", t % 2)], semkey=("xt", t % 2))
                for h in range(8):
                    S.act(lambda e, h=h: e.activation(out=hgb[:, h * 128:(h + 1) * 128], in_=hh[a][:, h, :], func=AF.Square, accum_out=ss8[:, h:h + 1]),
                          reads=[("hh", a, h % 2), "hgbfree"], writes=[("ss8", h)])

            def P3n(t):
                a = t % 2
                ss8, rs8 = smB[:, 2, :], smB[:, 3, :]
                HH = [("hh", a, 0), ("hh", a, 1)]
                S.pool(lambda e: e.tensor_scalar(out=ss8, in0=ss8, scalar1=1.0 / 128, scalar2=1e-6, op0=ALU.mult, op1=ALU.add),
                       reads=[("ss8", h) for h in range(8)], writes=["ss8v"])
                S.pool(lambda e: e.tensor_tensor(out=rs8, in0=ss8, in1=mhalf[:, 0:8], op=ALU.pow), reads=["ss8v", "mhalf"], writes=["rs8"])
                S.dve(lambda e: e.tensor_tensor(out=hh[a], in0=hh[a], in1=rs8.unsqueeze(2).to_broadcast([128, 8, 128]), op=ALU.mult), reads=HH + ["rs8"], writes=HH)

            def P3o(t):
                tok = slice(t * 128, (t + 1) * 128)
                XT = [("xnT_all", t)]
                for half in range(2):
                    for k in range(8):
                        S.pe(lambda e, half=half, k=k: e.matmul(banks[2 + half][:], lhsT=xnT_all[:, k, tok], rhs=wvog[:, k, 1024 + half * 512:1024 + (half + 1) * 512],
                                                               start=(k == 0), stop=(k == 7)), reads=WVOG + XT, writes=[("bank", 2 + half)])
                    S.act(lambda e, half=half: e.activation(out=osig[:, half * 512:(half + 1) * 512], in_=banks[2 + half][:], func=AF.Sigmoid),
                          reads=[("bank", 2 + half)], writes=[("osig", half)] + (["cwn"] if t == 0 else []))
                S.pool(lambda e: e.tensor_tensor(out=osig, in0=osig, in1=hgain, op=ALU.mult), reads=[("osig", 0), ("osig", 1), "hgain"], writes=["og"])

            def P3h(t):
                a = t % 2
                HH = [("hh", a, 0), ("hh", a, 1)]
                S.dve(lambda e: e.tensor_tensor(out=hgb, in0=hh[a].rearrange("p h d -> p (h d)"), in1=osig, op=ALU.mult),
                      reads=HH + ["og"] + [("ss8", h) for h in range(8)], writes=["hgb"])

            def P3b(t):
                a = t % 2
                sl = t % 2
                tok = slice(t * 128, (t + 1) * 128)
                pb7 = bank_bf(7)
                if pend[0] is not None:
                    pend[0]()
                    pend[0] = None
                for ch in range(8):
                    S.pe(lambda e, ch=ch: e.transpose(out=pb7[:, ch * 128:(ch + 1) * 128], in_=hgb[:, ch * 128:(ch + 1) * 128], identity=identb),
                         reads=["hgb", "identb"], writes=[("bank", 7)])
                S.act(lambda e: e.copy(out=hgT, in_=pb7.rearrange("p (k t) -> p k t", k=8)), reads=[("bank", 7), "hgb"], writes=["hgT", "hgbfree"])
                for half in range(2):
                    for ch in range(8):
                        S.pe(lambda e, half=half, ch=ch: e.matmul(banks[half][:], lhsT=hgT[:, ch, :], rhs=wo[:, ch, half * 512:(half + 1) * 512],
                                                                 start=(ch == 0), stop=(ch == 7)), reads=["hgT"] + WO, writes=[("bank", half)])
                pend[0] = residual_epilogue(t, (0, 1), xts[sl], ("xt", sl), h_out_d, xnT_all[:, :, tok], ("xnT_all", t), defer=True)

            print('arena M2', A.off * 4 / 1024, flush=True)

            NTL = 32
            P1a(0)
            P1b(0)
            P1g(0)
            P1c(0)
            for i in range(NTL + 1):
                if 1 <= i:
                    P3sq(i - 1)
                if i + 1 < NTL:
                    P1a(i + 1)
                if 1 <= i:
                    P3n(i - 1)
                    P3o(i - 1)
                if i + 1 < NTL:
                    P1b(i + 1)
                    P1g(i + 1)
                if i < NTL:
                    hb = P2(i)
                    hb(0)
                    hb(1)
                if 1 <= i:
                    P3h(i - 1)
                    P3b(i - 1)
                if i + 1 < NTL:
                    P1c(i + 1)
            if pend[0] is not None:
                pend[0]()
                pend[0] = None


        stage_A()
        if n_stage >= 2:
            stage_B(attn_w_out, oT_d, ffn_norm[0:1, :], x_d, hA_d)
        if n_stage >= 3:
            stage_F(0, mlstm_norm, hA_d, hB_d, False)
        if n_stage >= 4:
            stage_M(ffn_norm[1:2, :], hB_d, hA_d)
        if n_stage >= 5:
            stage_F(1, final_norm, hA_d, None, True)
        S.emit()
    return nc


def _consts():
    k = np.arange(128)[:, None]
    q = np.arange(128)[None, :]
    mask2 = np.concatenate([(q >= k), (k >= q)], axis=1).astype(np.float32)
    invf = (np.float32(500000.0) ** (-(np.arange(16, dtype=np.float32) * np.float32(2.0) / np.float32(32.0)))).astype(np.float32)
    return {"ident": np.eye(128, dtype=np.float32), "mask2": mask2,
            "invf": np.ascontiguousarray(np.broadcast_to(invf[None, :], (128, 16))).astype(np.float32)}


def make_in_maps(inputs, cores):
    f = lambda a: np.ascontiguousarray(np.asarray(a))
    shared = {
        "attn_norm": f(inputs["attn_norm"]).reshape(1, 1024),
        "attn_w_in": f(inputs["attn_w_in"]).reshape(1024, 9216),
        "attn_w_out": f(inputs["attn_w_out"]).reshape(1024, 1024),
        "mlstm_norm": f(inputs["mlstm_norm"]).reshape(1, 1024),
        "mlstm_w_in": f(inputs["mlstm_w_in"]).reshape(1024, 3088),
        "conv_w": f(inputs["mlstm_conv_w"]).reshape(4, 1024),
        "conv_b": f(inputs["mlstm_conv_b"]).reshape(1, 1024),
        "ig_bias": f(inputs["mlstm_ig_bias"]).reshape(1, 8),
        "fg_bias": f(inputs["mlstm_fg_bias"]).reshape(1, 8),
        "head_gain": f(inputs["mlstm_head_gain"]).reshape(1, 1024),
        "mlstm_w_out": f(inputs["mlstm_w_out"]).reshape(1024, 1024),
        "ffn_norm": f(inputs["ffn_norm"]).reshape(2, 1024),
        "ffn_w_in": f(inputs["ffn_w_in"]).reshape(2, 1024, 2 * DFF),
        "ffn_w_out": f(inputs["ffn_w_out"]).reshape(2, DFF, 1024),
        "final_norm": f(inputs["final_norm"]).reshape(1, 1024),
    }
    shared.update(_consts())
    x = f(inputs["x"])
    pos = f(inputs["positions"]).astype(np.int32)
    maps = []
    for c in cores:
        m = dict(shared)
        m["x"] = np.ascontiguousarray(x[2 * c:2 * c + 2].reshape(4096, 1024))
        m["pos"] = np.ascontiguousarray(pos[2 * c:2 * c + 2])
        maps.append(m)
    return maps


def kernel(**inputs):
    nc = build_program()
    in_maps = make_in_maps(inputs, list(range(8)))
    res = run_bass_kernel_spmd(nc, in_maps, core_ids=list(range(8)))
    outs = [np.asarray(r["out"]).reshape(2, 2048, 1024) for r in res.results]
    return np.concatenate(outs, axis=0).astype(np.float32)
```
